# Optimizing a Trainium2 kernel written in Bass

```python
import math
import jax, jax.numpy as jnp
from jax import lax
import numpy as np

D_MODEL = 2048
BATCH = 1
SEQ = 16384
DEPTH = 1

D_MIX = D_MODEL
A_HEADS = 4
A_HEAD_DIM = 128
A_VDIM = 2 * A_HEAD_DIM
A_WIDTH = A_HEADS * A_VDIM
Q_BLOCK = 128
M_HEADS = 4
M_QK_DIM = 128
M_VDIM = 256
M_WIDTH = M_HEADS * M_VDIM
M_CHUNK = 128
CONV_K = 4
GATE_CAP = 15.0
D_FF = 256 * ((8 * D_MODEL // 3 + 255) // 256)
EPS = 1e-6

SPLIT_SIZES = [
    A_HEADS * 2 * A_HEAD_DIM,
    A_HEADS * 2 * A_HEAD_DIM,
    A_WIDTH,
    M_HEADS * M_QK_DIM,
    M_HEADS * M_QK_DIM,
    M_WIDTH,
    M_WIDTH,
    M_HEADS,
    M_HEADS,
]
N_IN = sum(SPLIT_SIZES)

kernel_name = 'hymba_style_diffattn_mlstm_macaron'


def rmsnorm(x, g):
    x32 = x.astype(jnp.float32)
    y = x32 * lax.rsqrt(jnp.mean(x32 * x32, axis=-1, keepdims=True) + EPS)
    return (y * g.astype(jnp.float32)).astype(x.dtype)


def swiglu(h, w_gate, w_up, w_down):
    return (jax.nn.silu(h @ w_gate) * (h @ w_up)) @ w_down


def diff_attention(q, k, v, lam, head_gain, lam_init):
    B, S = q.shape[0], q.shape[1]
    nb = S // Q_BLOCK
    q = q * (A_HEAD_DIM ** -0.5)
    qb = q.reshape(B, nb, Q_BLOCK, A_HEADS, 2, A_HEAD_DIM).transpose(1, 0, 2, 3, 4, 5)
    kpos = jnp.arange(S)

    def block(args):
        qi, i = args
        s = jnp.einsum('bqhcd,bkhcd->bhcqk', qi, k).astype(jnp.float32)
        qpos = i * Q_BLOCK + jnp.arange(Q_BLOCK)
        mask = kpos[None, :] <= qpos[:, None]
        a = jax.nn.softmax(jnp.where(mask, s, -jnp.inf), axis=-1)
        w = a[:, :, 0] - lam * a[:, :, 1]
        return jnp.einsum('bhqk,bkhe->bqhe', w.astype(v.dtype), v)

    o = lax.map(block, (qb, jnp.arange(nb)))
    o = o.transpose(1, 0, 2, 3, 4).reshape(B, S, A_HEADS, A_VDIM)
    o = rmsnorm(o, head_gain) * (1.0 - lam_init)
    return o.reshape(B, S, A_WIDTH)


def causal_dwconv(x, w, b):
    y = lax.conv_general_dilated(
        x, w.astype(x.dtype), window_strides=(1,), padding=[(CONV_K - 1, 0)],
        dimension_numbers=('NWC', 'WIO', 'NWC'), feature_group_count=x.shape[-1])
    return y + b.astype(x.dtype)


def mlstm_chunkwise(q, k, v, i_pre, f_pre):
    B, S = q.shape[0], q.shape[1]
    nc = S // M_CHUNK
    L = M_CHUNK
    q = q.astype(jnp.float32) * (M_QK_DIM ** -0.5)
    k = k.astype(jnp.float32)
    v = v.astype(jnp.float32)

    def to_chunks(t):
        return t.reshape(B, nc, L, M_HEADS, t.shape[-1]).transpose(1, 0, 3, 2, 4)

    def gate_chunks(t):
        return t.reshape(B, nc, L, M_HEADS).transpose(1, 0, 3, 2)

    log_f = jax.nn.log_sigmoid(f_pre)
    xs = (to_chunks(q), to_chunks(k), to_chunks(v), gate_chunks(i_pre), gate_chunks(log_f))
    tri = jnp.tril(jnp.ones((L, L), dtype=bool))

    def step(carry, inp):
        C, n, m = carry
        qc, kc, vc, ig, lf = inp
        b = jnp.cumsum(lf, axis=-1)
        d_log = b[..., :, None] - b[..., None, :] + ig[..., None, :]
        d_log = jnp.where(tri, d_log, -jnp.inf)
        inter = b + m[..., None]
        m_t = jnp.maximum(inter, jnp.max(d_log, axis=-1))
        s_w = jnp.einsum('bhtd,bhsd->bhts', qc, kc) * jnp.exp(d_log - m_t[..., None])
        g = jnp.exp(inter - m_t)
        num = g[..., None] * jnp.einsum('bhtd,bhde->bhte', qc, C) + jnp.einsum('bhts,bhse->bhte', s_w, vc)
        den = g * jnp.einsum('bhtd,bhd->bht', qc, n) + jnp.sum(s_w, axis=-1)
        den = jnp.maximum(jnp.abs(den), jnp.exp(-m_t))
        h = num / den[..., None]
        b_last = b[..., -1]
        w_log = b_last[..., None] - b + ig
        m_new = jnp.maximum(b_last + m, jnp.max(w_log, axis=-1))
        ws = jnp.exp(w_log - m_new[..., None])
        g_state = jnp.exp(b_last + m - m_new)
        C_new = g_state[..., None, None] * C + jnp.einsum('bhs,bhsd,bhse->bhde', ws, kc, vc)
        n_new = g_state[..., None] * n + jnp.einsum('bhs,bhsd->bhd', ws, kc)
        return (C_new, n_new, m_new), h

    init = (jnp.zeros((B, M_HEADS, M_QK_DIM, M_VDIM), jnp.float32),
            jnp.zeros((B, M_HEADS, M_QK_DIM), jnp.float32),
            jnp.zeros((B, M_HEADS), jnp.float32))
    _, h = lax.scan(step, init, xs)
    return h.transpose(1, 0, 3, 2, 4).reshape(B, S, M_HEADS, M_VDIM)


def setup_inputs(seed: int = 0) -> dict:
    key = jax.random.key(seed)
    ks = jax.random.split(key, 24)
    f32 = jnp.float32

    def nrm(k, shape, scale):
        return jax.random.normal(k, shape, f32) * scale

    def gain(k, shape):
        return 1.0 + 0.02 * jax.random.normal(k, shape, f32)

    f_bias = jnp.linspace(3.0, 6.0, M_HEADS, dtype=f32)[None, :] + nrm(ks[13], (DEPTH, M_HEADS), 0.1)
    return {
        'x': nrm(ks[0], (BATCH, SEQ, D_MODEL), 1.0),
        'ffn1_norm': gain(ks[1], (DEPTH, D_MODEL)),
        'ffn1_w_gate': nrm(ks[2], (DEPTH, D_MODEL, D_FF), D_MODEL ** -0.5),
        'ffn1_w_up': nrm(ks[3], (DEPTH, D_MODEL, D_FF), D_MODEL ** -0.5),
        'ffn1_w_down': nrm(ks[4], (DEPTH, D_FF, D_MODEL), D_FF ** -0.5),
        'mix_norm': gain(ks[5], (DEPTH, D_MODEL)),
        'w_in': nrm(ks[6], (DEPTH, D_MODEL, N_IN), D_MODEL ** -0.5),
        'lam_q1': nrm(ks[7], (DEPTH, A_HEAD_DIM), 0.1),
        'lam_k1': nrm(ks[8], (DEPTH, A_HEAD_DIM), 0.1),
        'lam_q2': nrm(ks[9], (DEPTH, A_HEAD_DIM), 0.1),
        'lam_k2': nrm(ks[10], (DEPTH, A_HEAD_DIM), 0.1),
        'attn_head_norm': gain(ks[11], (DEPTH, A_HEADS, A_VDIM)),
        'conv_w': nrm(ks[12], (DEPTH, CONV_K, 1, 2 * M_HEADS * M_QK_DIM), CONV_K ** -0.5),
        'conv_b': nrm(ks[14], (DEPTH, 2 * M_HEADS * M_QK_DIM), 0.01),
        'b_igate': nrm(ks[15], (DEPTH, M_HEADS), 0.1),
        'b_fgate': f_bias,
        'mlstm_head_norm': gain(ks[16], (DEPTH, M_HEADS, M_VDIM)),
        'w_out': nrm(ks[17], (DEPTH, D_MIX, D_MODEL), D_MIX ** -0.5),
        'ffn2_norm': gain(ks[18], (DEPTH, D_MODEL)),
        'ffn2_w_gate': nrm(ks[19], (DEPTH, D_MODEL, D_FF), D_MODEL ** -0.5),
        'ffn2_w_up': nrm(ks[20], (DEPTH, D_MODEL, D_FF), D_MODEL ** -0.5),
        'ffn2_w_down': nrm(ks[21], (DEPTH, D_FF, D_MODEL), D_FF ** -0.5),
        'final_norm': gain(ks[22], (D_MODEL,)),
    }


def reference(x, ffn1_norm, ffn1_w_gate, ffn1_w_up, ffn1_w_down, mix_norm, w_in,
              lam_q1, lam_k1, lam_q2, lam_k2, attn_head_norm, conv_w, conv_b,
              b_igate, b_fgate, mlstm_head_norm, w_out, ffn2_norm, ffn2_w_gate,
              ffn2_w_up, ffn2_w_down, final_norm):
    B, S, _ = x.shape
    split_idx = list(np.cumsum(SPLIT_SIZES)[:-1])
    for l in range(DEPTH):
        x = x + 0.5 * swiglu(rmsnorm(x, ffn1_norm[l]), ffn1_w_gate[l], ffn1_w_up[l], ffn1_w_down[l])

        h = rmsnorm(x, mix_norm[l])
        a_q, a_k, a_v, m_q, m_k, m_v, m_o, m_i, m_f = jnp.split(h @ w_in[l], split_idx, axis=-1)

        lam_init = 0.8 - 0.6 * math.exp(-0.3 * l)
        lam = (jnp.exp(jnp.sum(lam_q1[l].astype(jnp.float32) * lam_k1[l].astype(jnp.float32)))
               - jnp.exp(jnp.sum(lam_q2[l].astype(jnp.float32) * lam_k2[l].astype(jnp.float32)))
               + lam_init)
        y_a = diff_attention(
            a_q.reshape(B, S, A_HEADS, 2, A_HEAD_DIM),
            a_k.reshape(B, S, A_HEADS, 2, A_HEAD_DIM),
            a_v.reshape(B, S, A_HEADS, A_VDIM),
            lam, attn_head_norm[l], lam_init)

        qk = jax.nn.silu(causal_dwconv(jnp.concatenate([m_q, m_k], axis=-1), conv_w[l], conv_b[l]))
        mq, mk = jnp.split(qk, 2, axis=-1)
        i_pre = m_i.astype(jnp.float32) + b_igate[l].astype(jnp.float32)
        f_pre = m_f.astype(jnp.float32) + b_fgate[l].astype(jnp.float32)
        i_pre = GATE_CAP * jnp.tanh(i_pre / GATE_CAP)
        f_pre = GATE_CAP * jnp.tanh(f_pre / GATE_CAP)
        hm = mlstm_chunkwise(
            mq.reshape(B, S, M_HEADS, M_QK_DIM),
            mk.reshape(B, S, M_HEADS, M_QK_DIM),
            m_v.reshape(B, S, M_HEADS, M_VDIM),
            i_pre, f_pre)
        hm = rmsnorm(hm, mlstm_head_norm[l]).reshape(B, S, M_WIDTH).astype(x.dtype)
        y_b = hm * jax.nn.sigmoid(m_o)

        x = x + jnp.concatenate([y_a, y_b], axis=-1) @ w_out[l]

        x = x + 0.5 * swiglu(rmsnorm(x, ffn2_norm[l]), ffn2_w_gate[l], ffn2_w_up[l], ffn2_w_down[l])
    return rmsnorm(x, final_norm)
```

```python
import math
import numpy as np
import concourse.bass as bass
import concourse.mybir as mybir
from concourse.bass_utils import run_bass_kernel_spmd

F32 = mybir.dt.float32
BF16 = mybir.dt.bfloat16
AF = mybir.ActivationFunctionType
ALU = mybir.AluOpType

NCORES = 8
N_IN = 6152
EPS = 1e-6
SB_LO = 16512
SB_HI = 229344
QSCALE = 128 ** -0.5
LAM_INIT = 0.8 - 0.6 * math.exp(0.0)

def cst_layout(DC):
    o = {}
    p = 0
    for nm, n in [("g_ffn1", DC), ("g_mix", DC), ("g_ffn2", DC), ("g_final", DC), ("conv_w", 32), ("conv_b", 8),
                  ("ahn", 8), ("mhn", 8), ("lam", 4), ("sel", 8), ("selc", 8), ("bgate", 2), ("eps", 1), ("one", 1),
                  ("lng", 1)]:
        o[nm] = p
        p += n
    o["_n"] = p
    return o


class Cfg:
    def __init__(self, S, D, DFF):
        self.S, self.D, self.DFF = S, D, DFF
        self.NTOK = S // NCORES
        self.NB = self.NTOK // 128
        self.TT = 512
        self.NT = self.NTOK // self.TT
        self.DC = D // 128
        self.FC = DFF // 128
        self.NG = self.NB // 4
        assert self.NTOK % 512 == 0 and D % 128 == 0 and DFF % 128 == 0


class EngS:
    def __init__(self, name, eng, sem, semi):
        self.name, self.eng, self.sem, self.semi = name, eng, sem, semi
        self.count = 0
        self.waited = {}


class FW:
    def __init__(self, nc):
        self.nc = nc
        self.sems = []
        self.engs = {}
        for nm, e in [("pe", nc.tensor), ("act", nc.scalar), ("dve", nc.vector), ("pool", nc.gpsimd), ("sp", nc.sync)]:
            s = self.newsem("e_" + nm)
            self.engs[nm] = EngS(nm, e, self.sems[s], s)
        self.regions = {}
        self.streams = {}
        self.lat = {}
        self.ccsems = set()

    def newsem(self, name):
        self.sems.append(self.nc.alloc_semaphore(name))
        return len(self.sems) - 1

    def _wait(self, E, tok):
        si, val = tok
        if E.waited.get(si, 0) >= val:
            return
        E.eng.wait_ge(self.sems[si], val)
        E.waited[si] = val

    def _deps(self, E, r, w):
        need = {}
        for k in r:
            reg = self.regions.get(k)
            if reg:
                for si, v in reg[0].items():
                    if need.get(si, 0) < v:
                        need[si] = v
        for k in w:
            reg = self.regions.get(k)
            if reg:
                for d in reg:
                    for si, v in d.items():
                        if need.get(si, 0) < v:
                            need[si] = v
        for si, v in need.items():
            if si == E.semi and E.name == "pe":
                continue
            self._wait(E, (si, v))

    def _upd(self, r, w, tok):
        si, v = tok
        self.lat[si] = max(self.lat.get(si, 0), v)
        for k in r:
            reg = self.regions.setdefault(k, [{}, {}])
            if reg[1].get(si, 0) < v:
                reg[1][si] = v
        for k in w:
            self.regions[k] = [{si: v}, {}]

    def op(self, e, fn, r=(), w=()):
        E = self.engs[e]
        self._deps(E, r, w)
        inst = fn(E.eng)
        E.count += 1
        inst.then_inc(E.sem, 1)
        self._upd(r, w, (E.semi, E.count))

    def dma(self, q, out, in_, r=(), w=(), stream="d", R=8):
        Q = self.engs[q]
        self._deps(Q, r, w)
        st = self.streams.get(stream)
        if st is None:
            st = {"sems": [self.newsem(f"s_{stream}_{i}") for i in range(R)], "n": 0}
            self.streams[stream] = st
        R = len(st["sems"])
        i = st["n"]
        st["n"] += 1
        si = st["sems"][i % R]
        if i >= R:
            self._wait(Q, (si, 16 * (i // R)))
        Q.eng.dma_start(out=out, in_=in_).then_inc(self.sems[si], 16)
        self._upd(r, w, (si, 16 * (i // R + 1)))

    def allgather(self, ins, outs, r=(), w=()):
        Q = self.engs["pool"]
        self._deps(Q, r, w)
        si = self.newsem(f"cc{len(self.sems)}")
        self.ccsems.add(si)
        Q.eng.collective_compute("AllGather", ALU.bypass, replica_groups=[list(range(NCORES))],
                                 ins=[ins.opt()], outs=[outs.opt()]).then_inc(self.sems[si])
        self._upd(r, w, (si, 1))

    def barrier(self, keep=()):
        saved = {k: self.regions[k] for k in keep if k in self.regions}
        skip = set()
        for reg in saved.values():
            for d in reg:
                skip.update(k for k in d.keys() if k in self.ccsems)
        for E in self.engs.values():
            for si, v in self.lat.items():
                if si == E.semi or si in skip:
                    continue
                self._wait(E, (si, v))
        self.regions = dict(saved)


class Ring:
    def __init__(self, fw, name, nslots):
        self.fw, self.name, self.n = fw, name, nslots
        self.tasks = []
        self.issued = 0

    def add(self, fn):
        self.tasks.append(fn)
        return len(self.tasks) - 1

    def prefetch(self, upto):
        upto = min(upto, len(self.tasks) - 1)
        while self.issued <= upto:
            k = self.issued
            self.tasks[k](k % self.n)
            self.issued += 1

    def use(self, k, limit=None):
        up = k + self.n - 1
        if limit is not None:
            up = min(up, limit)
        self.prefetch(up)
        return k % self.n

    def key(self, slot):
        return (self.name, slot)


class SB:
    def __init__(self, nc):
        self.nc = nc
        self.p = SB_LO
        self.i = 0

    def alloc(self, name, shape, dt):
        n = 1
        for s in shape[1:]:
            n *= s
        nbytes = n * (4 if dt == F32 else 2)
        off = (self.p + 63) // 64 * 64
        assert off + nbytes <= SB_HI, f"SBUF overflow at {name}: {off + nbytes}"
        self.p = off + nbytes
        self.i += 1
        return self.nc.alloc_sbuf_tensor_at(f"{name}_{self.i}", list(shape), dt, offset=off)

    def mark(self):
        return self.p

    def reset(self, m):
        self.p = m


def build(cfg, debug=False):
    S, D, DFF, NTOK, NB, TT, NT, DC, FC, NG = (cfg.S, cfg.D, cfg.DFF, cfg.NTOK, cfg.NB, cfg.TT, cfg.NT, cfg.DC,
                                                cfg.FC, cfg.NG)
    nc = bass.Bass("TRN2", target_bir_lowering=False)
    fw = FW(nc)
    sb = SB(nc)
    CL = cst_layout(DC)

    def din(name, shape, dt=F32):
        return nc.dram_tensor(name, list(shape), dt, kind="ExternalInput").ap()

    xT = din("xT", [D, NTOK])
    wg = [din("wg1", [D, DFF]), din("wg2", [D, DFF])]
    wu = [din("wu1", [D, DFF]), din("wu2", [D, DFF])]
    wd = [din("wd1", [DFF, D]), din("wd2", [DFF, D])]
    win = din("win", [D, N_IN])
    wout = din("wout", [2048, D])
    cst_d = din("cst", [128, CL["_n"]])
    cmat_d = din("cmat", [128, 3 * 128 + 512])
    cmatb_d = din("cmatb", [128, 2 * 128 + 8 * 128])
    outT = nc.dram_tensor("outT", [D, NTOK], F32, kind="ExternalOutput").ap()

    dbg = {}

    def scratch(name, shape, dt, cc=False):
        if debug and not cc:
            t = nc.dram_tensor(name, list(shape), dt, kind="ExternalOutput")
            dbg[name] = True
        else:
            t = nc.dram_tensor(name, list(shape), dt)
        return t.ap()

    X1 = scratch("X1", [D, NTOK], F32)
    QT = scratch("QT", [1024, NTOK], BF16)
    KT_loc = [scratch(f"KT_loc{t}", [1024, TT], BF16, cc=True) for t in range(NT)]
    KT_all = [scratch(f"KT_all{t}", [8 * 1024, TT], BF16, cc=True) for t in range(NT)]
    V_loc = [scratch(f"V_loc{t}", [TT, 1024], BF16, cc=True) for t in range(NT)]
    V_all = [scratch(f"V_all{t}", [8 * TT, 1024], BF16, cc=True) for t in range(NT)]
    kvkeys = [("KT_all", t) for t in range(NT)] + [("V_all", t) for t in range(NT)]
    MQK = scratch("MQK", [1024, NTOK], F32)
    MV = scratch("MV", [NTOK, 1024], BF16)
    MO = scratch("MO", [1024, NTOK], BF16)
    TL_loc = scratch("TL_loc", [128, 8 * NB * 3], F32, cc=True)
    TL_all = scratch("TL_all", [8 * 128, 8 * NB * 3], F32, cc=True)
    CHS_loc = scratch("CHS_loc", [4, NB * 2], F32, cc=True)
    CHS_all = scratch("CHS_all", [32, NB * 2], F32, cc=True)
    U_loc = scratch("U_loc", [NB * 4 * 128, 257], F32, cc=True)
    U_all = scratch("U_all", [8 * NB * 4 * 128, 257], F32, cc=True)
    YA = scratch("YA", [1024, NTOK], BF16)
    YB = scratch("YB", [1024, NTOK], BF16)

    banks = [nc.alloc_psum_tensor(f"bank{i}", [128, 512], F32) for i in range(8)]

    def pk(i):
        return ("ps", i)

    cst = sb.alloc("cst", [128, CL["_n"]], F32)
    cmat = sb.alloc("cmat", [128, 3 * 128 + 512], F32)
    cmatb = sb.alloc("cmatb", [128, 10 * 128], BF16)
    lamt = sb.alloc("lamt", [128, 8], F32)
    hole_lo = sb.mark()
    GI = sb.alloc("GI", [4, NTOK], F32)
    GF = sb.alloc("GF", [4, NTOK], F32)
    tails_sb = sb.alloc("tails", [128, 8, NB, 3], F32)
    hole_hi = sb.mark()
    fw.dma("sp", cst[:], cst_d, w=["cst"], stream="io")
    fw.dma("sp", cmat[:], cmat_d, w=["cmat"], stream="io")
    fw.dma("pool", cmatb[:], cmatb_d, w=["cmatb"], stream="wq")
    ident_f = cmat[:, 0:128]
    trimask = cmat[:, 128:256]
    ones_f = cmat[:, 256:384]
    ident_b = cmatb[:, 0:128]
    ones_b = cmatb[:, 128:256]

    def ccol(name, i=0):
        return cst[:, CL[name] + i:CL[name] + i + 1]

    lo = CL["lam"]
    fw.op("dve", lambda e: e.tensor_tensor(out=lamt[:, 0:1], in0=cst[:, lo:lo + 1], in1=cst[:, lo + 1:lo + 2],
                                           op=ALU.mult), r=["cst"], w=["lamt"])
    fw.op("dve", lambda e: e.tensor_tensor(out=lamt[:, 1:2], in0=cst[:, lo + 2:lo + 3], in1=cst[:, lo + 3:lo + 4],
                                           op=ALU.mult), r=["cst", "lamt"], w=["lamt"])
    fw.op("pe", lambda e: e.matmul(banks[7][:, 0:2], ones_f, lamt[:, 0:2], start=True, stop=True),
          r=["cmat", "lamt"], w=[pk(7)])
    fw.op("act", lambda e: e.activation(out=lamt[:, 2:4], in_=banks[7][:, 0:2], func=AF.Exp), r=[pk(7), "lamt"],
          w=["lamt"])
    fw.op("dve", lambda e: e.tensor_tensor(out=lamt[:, 5:6], in0=lamt[:, 3:4], in1=lamt[:, 2:3], op=ALU.subtract),
          r=["lamt"], w=["lamt"])
    fw.op("dve", lambda e: e.tensor_scalar(out=lamt[:, 4:5], in0=lamt[:, 5:6], scalar1=-LAM_INIT, scalar2=None,
                                           op0=ALU.add), r=["lamt"], w=["lamt"])
    neglam = lamt[:, 4:5]
    pmark = sb.mark()

    def rmsnorm_stats(xt, T, rstd, sq, nfeat, bank, xkeys, tag):
        nch = len(xkeys)
        for c in range(nch):
            fw.op("act", lambda e, c=c: e.activation(out=sq[:, c % 2, :], in_=xt[:, c, :], func=AF.Square),
                  r=[xkeys[c]], w=[(tag + "sq", c % 2)])
            fw.op("pe", lambda e, c=c: e.matmul(banks[bank][:, 0:T], ones_b, sq[:, c % 2, :], start=(c == 0),
                                                stop=(c == nch - 1)),
                  r=[(tag + "sq", c % 2), "cmatb"], w=[pk(bank)])
        fw.op("act", lambda e: e.activation(out=rstd[:, 0:T], in_=banks[bank][:, 0:T], func=AF.Sqrt,
                                            bias=ccol("eps"), scale=1.0 / nfeat),
              r=[pk(bank), "cst"], w=[tag + "rstd"])
        fw.op("dve", lambda e: e.reciprocal(out=rstd[:, 0:T], in_=rstd[:, 0:T]), r=[tag + "rstd"], w=[tag + "rstd"])

    class FFNBufs:
        pass

    def alloc_ffn():
        b = FFNBufs()
        b.xt = sb.alloc("xt", [128, DC, TT], F32)
        b.h = sb.alloc("h", [128, DC, TT], BF16)
        b.act = sb.alloc("act", [128, FC, TT], BF16)
        b.sq = sb.alloc("sq", [128, 2, TT], BF16)
        b.rstd = sb.alloc("rstd", [128, TT], F32)
        b.sg = sb.alloc("sg", [128, 2, TT], F32)
        b.wgu = [sb.alloc(f"wgu{i}", [128, max(DC, 16), 2, 128], BF16) for i in range(3)]
        b.wd = [sb.alloc(f"wd{i}", [128, FC, 128], BF16) for i in range(2)]
        b.rg = Ring(fw, "wgu", 3)
        b.rd = Ring(fw, "wd", 2)
        return b

    def norm_to_h(b, gname):
        xk = [("x", c) for c in range(DC)]
        rmsnorm_stats(b.xt, TT, b.rstd, b.sq, D, 6, xk, "n")
        for c in range(DC):
            fw.op("dve", lambda e, c=c: e.scalar_tensor_tensor(out=b.h[:, c, :], in0=b.xt[:, c, :],
                                                               scalar=ccol(gname, c), in1=b.rstd[:, :],
                                                               op0=ALU.mult, op1=ALU.mult),
                  r=[("x", c), "nrstd", "cst"], w=[("h", c)])

    def plan_ffn(b, l):
        k0 = len(b.rg.tasks)
        for f in range(FC):
            def ld(slot, f=f):
                fw.dma("pool", b.wgu[slot][:, 0:DC, 0, :],
                       wg[l][:, f * 128:(f + 1) * 128].rearrange("(c p) n -> p c n", p=128),
                       w=[("wgu", slot, 0)], stream="wq")
                fw.dma("pool", b.wgu[slot][:, 0:DC, 1, :],
                       wu[l][:, f * 128:(f + 1) * 128].rearrange("(c p) n -> p c n", p=128),
                       w=[("wgu", slot, 1)], stream="wq")
            b.rg.add(ld)
        k1 = len(b.rd.tasks)
        for o in range(DC):
            def ld(slot, o=o):
                fw.dma("pool", b.wd[slot][:, :, :],
                       wd[l][:, o * 128:(o + 1) * 128].rearrange("(c p) n -> p c n", p=128),
                       w=[("wd", slot)], stream="wq")
            b.rd.add(ld)
        return k0, k1

    def run_ffn(b, k0, k1):
        hk = [("h", c) for c in range(DC)]
        b.rd.prefetch(k1)
        for f in range(FC):
            slot = b.rg.use(k0 + f)
            gb, ub = f % 2, 2 + f % 2
            for (bank, j) in ((gb, 0), (ub, 1)):
                def mm(e, bank=bank, j=j, slot=slot):
                    for c in range(DC):
                        ins = e.matmul(banks[bank][:, 0:TT], b.wgu[slot][:, c, j, :], b.h[:, c, :], start=(c == 0),
                                       stop=(c == DC - 1))
                    return ins
                fw.op("pe", mm, r=hk + [("wgu", slot, j)], w=[pk(bank)])
            fw.op("act", lambda e, f=f, gb=gb: e.activation(out=b.sg[:, f % 2, :], in_=banks[gb][:, 0:TT],
                                                            func=AF.Silu), r=[pk(gb)], w=[("sg", f % 2)])
            fw.op("dve", lambda e, f=f, ub=ub: e.tensor_tensor(out=b.act[:, f, :], in0=b.sg[:, f % 2, :],
                                                               in1=banks[ub][:, 0:TT], op=ALU.mult),
                  r=[("sg", f % 2), pk(ub)], w=[("act", f)])
        ak = [("act", f) for f in range(FC)]
        for o in range(DC):
            slot = b.rd.use(k1 + o)
            bank = 4 + o % 2

            def mm(e, bank=bank, slot=slot):
                for f in range(FC):
                    ins = e.matmul(banks[bank][:, 0:TT], b.wd[slot][:, f, :], b.act[:, f, :], start=(f == 0),
                                   stop=(f == FC - 1))
                return ins
            fw.op("pe", mm, r=ak + [("wd", slot)], w=[pk(bank)])
            fw.op("dve", lambda e, o=o, bank=bank: e.scalar_tensor_tensor(out=b.xt[:, o, :], in0=banks[bank][:, 0:TT],
                                                                          scalar=0.5, in1=b.xt[:, o, :], op0=ALU.mult,
                                                                          op1=ALU.add),
                  r=[pk(bank), ("x", o)], w=[("x", o)])

    b = alloc_ffn()
    stb = [sb.alloc(f"stb{i}", [128, TT], BF16) for i in range(3)]
    stf = [sb.alloc(f"stf{i}", [128, TT], F32) for i in range(2)]
    if FC >= 32 and DC <= 16:
        wtm = [b.act[:, 16 * i:16 * (i + 1), :] for i in range(2)]
        wtm_al = [[("act", f) for f in range(16 * i, 16 * i + 16)] for i in range(2)]
    else:
        wtm = [sb.alloc(f"wtm{i}", [128, DC, 512], BF16)[:] for i in range(2)]
        wtm_al = [[], []]
    rt = Ring(fw, "wtm", 2)

    fm = []
    for i in range(8):
        fm.append((i * 128, 128, "aq", i))
    for i in range(8):
        fm.append((1024 + i * 128, 128, "ak", i))
    for i in range(8):
        fm.append((3072 + i * 128, 128, "mqk", i))
    for i in range(8):
        fm.append((5120 + i * 128, 128, "mo", i))
    fm.append((6144, 4, "gi", 0))
    fm.append((6148, 4, "gf", 0))
    tmg = [(2048, "av", 0), (2560, "av", 1), (4096, "mv", 0), (4608, "mv", 1)]

    plans = []
    for t in range(NT):
        k0, k1 = plan_ffn(b, 0)
        kp = len(b.rg.tasks)
        for (c0, M, kind, idx) in fm:
            def ld(slot, c0=c0, M=M):
                fw.dma("pool", b.wgu[slot][:, 0:DC, 0, 0:M], win[:, c0:c0 + M].rearrange("(c p) n -> p c n", p=128),
                       w=[("wgu", slot, 0)], stream="wq")
            b.rg.add(ld)
        kt = len(rt.tasks)
        for (c0, kind, idx) in tmg:
            def ld(slot, c0=c0):
                fw.dma("pool", wtm[slot][:, 0:DC, :], win[:, c0:c0 + 512].rearrange("(c p) n -> p c n", p=128),
                       w=[("wtm", slot)] + wtm_al[slot], stream="wq")
            rt.add(ld)
        plans.append((k0, k1, kp, kt))

    nstb = 0
    nstf = 0
    for t in range(NT):
        k0, k1, kp, kt = plans[t]
        tc0 = t * TT
        for c in range(DC):
            fw.dma("sp", b.xt[:, c, :], xT[c * 128:(c + 1) * 128, tc0:tc0 + TT], w=[("x", c)], stream="io")
        norm_to_h(b, "g_ffn1")
        run_ffn(b, k0, k1)
        rt.prefetch(kt)
        for c in range(DC):
            fw.dma("sp", X1[c * 128:(c + 1) * 128, tc0:tc0 + TT], b.xt[:, c, :], r=[("x", c)], w=[("X1", t, c)],
                   stream="io")
        norm_to_h(b, "g_mix")
        hk = [("h", c) for c in range(DC)]
        for q, (c0, M, kind, idx) in enumerate(fm):
            slot = b.rg.use(kp + q)
            bank = 4 + q % 2

            def mm(e, bank=bank, slot=slot, M=M):
                for c in range(DC):
                    ins = e.matmul(banks[bank][0:M, 0:TT], b.wgu[slot][:, c, 0, 0:M], b.h[:, c, :], start=(c == 0),
                                   stop=(c == DC - 1))
                return ins
            fw.op("pe", mm, r=hk + [("wgu", slot, 0)], w=[pk(bank)])
            if kind in ("aq", "ak", "mo"):
                s = nstb % 3
                nstb += 1
                if kind == "mo":
                    fw.op("act", lambda e, s=s, bank=bank: e.activation(out=stb[s][:, :], in_=banks[bank][:, 0:TT],
                                                                        func=AF.Sigmoid), r=[pk(bank)], w=[("stb", s)])
                elif kind == "aq":
                    fw.op("dve", lambda e, s=s, bank=bank: e.tensor_copy(out=stb[s][:, :], in_=banks[bank][:, 0:TT]),
                          r=[pk(bank)], w=[("stb", s)])
                else:
                    fw.op("act", lambda e, s=s, bank=bank: e.activation(out=stb[s][:, :], in_=banks[bank][:, 0:TT],
                                                                        func=AF.Copy), r=[pk(bank)], w=[("stb", s)])
                if kind == "ak":
                    dap = KT_loc[t][idx * 128:(idx + 1) * 128, :]
                else:
                    dap = {"aq": QT, "mo": MO}[kind][idx * 128:(idx + 1) * 128, tc0:tc0 + TT]
                fw.dma("sp", dap, stb[s][:, :], r=[("stb", s)], w=[(kind, t, idx)], stream="io")
            elif kind == "mqk":
                s = nstf % 2
                nstf += 1
                fw.op("dve", lambda e, s=s, bank=bank: e.tensor_copy(out=stf[s][:, :], in_=banks[bank][:, 0:TT]),
                      r=[pk(bank)], w=[("stf", s)])
                fw.op("dve", lambda e, s=s, idx=idx, t=t: e.tensor_copy(
                    out=tails_sb[:, idx, t * 4:(t + 1) * 4, :],
                    in_=stf[s][:, :].rearrange("p (b k) -> p b k", k=128)[:, :, 125:128]),
                    r=[("stf", s)], w=[("tails", t, idx)])
                fw.dma("sp", MQK[idx * 128:(idx + 1) * 128, tc0:tc0 + TT], stf[s][:, :], r=[("stf", s)],
                       w=[("mqk", t, idx)], stream="io")
            else:
                G = GI if kind == "gi" else GF
                bc = 0 if kind == "gi" else 1
                fw.op("dve", lambda e, bank=bank, G=G, bc=bc: e.tensor_scalar(
                    out=G[:, tc0:tc0 + TT], in0=banks[bank][0:4, 0:TT], scalar1=cst[0:4, CL["bgate"] + bc:CL["bgate"] + bc + 1],
                    scalar2=1.0 / 15.0, op0=ALU.add, op1=ALU.mult), r=[pk(bank), "cst"], w=[(kind, t)])
                fw.op("act", lambda e, G=G: e.activation(out=G[:, tc0:tc0 + TT], in_=G[:, tc0:tc0 + TT], func=AF.Tanh),
                      r=[(kind, t)], w=[(kind, t)])
        for q, (c0, kind, idx) in enumerate(tmg):
            slot = rt.use(kt + q, limit=kt + len(tmg) - 1)
            for blk in range(4):
                bank = 4 + (q * 4 + blk) % 2

                def mm(e, bank=bank, slot=slot, blk=blk):
                    for c in range(DC):
                        ins = e.matmul(banks[bank][:, 0:512], b.h[:, c, blk * 128:(blk + 1) * 128], wtm[slot][:, c, :],
                                       start=(c == 0), stop=(c == DC - 1))
                    return ins
                fw.op("pe", mm, r=hk + [("wtm", slot)] + wtm_al[slot], w=[pk(bank)])
                s = nstb % 3
                nstb += 1
                if blk % 2 == 0:
                    fw.op("act", lambda e, s=s, bank=bank: e.activation(out=stb[s][:, :], in_=banks[bank][:, 0:512],
                                                                        func=AF.Copy), r=[pk(bank)], w=[("stb", s)])
                else:
                    fw.op("dve", lambda e, s=s, bank=bank: e.tensor_copy(out=stb[s][:, :], in_=banks[bank][:, 0:512]),
                          r=[pk(bank)], w=[("stb", s)])
                if kind == "av":
                    dap = V_loc[t][blk * 128:(blk + 1) * 128, idx * 512:(idx + 1) * 512]
                else:
                    r0 = tc0 + blk * 128
                    dap = MV[r0:r0 + 128, idx * 512:(idx + 1) * 512]
                fw.dma("sp", dap, stb[s][:, :], r=[("stb", s)], w=[(kind, t, idx, blk)], stream="io")

        def kv_ag(t=t):
            fw.allgather(KT_loc[t], KT_all[t], r=[("ak", t, i) for i in range(8)], w=[("KT_all", t)])
            fw.allgather(V_loc[t], V_all[t], r=[("av", t, i, bl) for i in range(2) for bl in range(4)],
                         w=[("V_all", t)])
        if t < NT - 1:
            kv_ag()

    fw.dma("sp", TL_loc, tails_sb[:].rearrange("p a b c -> p (a b c)"),
           r=[("tails", t, i) for t in range(NT) for i in range(8)], w=["TL_loc"], stream="io")
    fw.allgather(TL_loc, TL_all, r=["TL_loc"], w=["TL_all"])

    lastkv = [("ak", NT - 1, i) for i in range(8)] + [("av", NT - 1, i, bl) for i in range(2) for bl in range(4)]
    fw.barrier(keep=kvkeys + lastkv)
    sb.reset(pmark)
    qk_m = sb.alloc("qk_m", [128, 8, NTOK], BF16)
    cols_tok = sb.alloc("cols_tok", [128, NB, 16], F32)
    gs_rep = sb.alloc("gs_rep", [128, 4, 128], F32)
    emark = sb.mark()

    def g4(name):
        return sb.alloc(name, [4, NB, 128], F32)
    sp_ = g4("sp")
    csp = g4("csp")
    a_ = g4("a")
    cm = g4("cm")
    Mt = g4("M")
    wsq = g4("ws")
    Eq = g4("E")
    gq = g4("g")
    clq = g4("cl")
    ones4 = sb.alloc("ones4", [4, 128], F32)
    chs = sb.alloc("chs", [4, NB, 2], F32)
    chs_all = sb.alloc("chs_all", [4, NB, 8, 2], F32)
    amax_g = sb.alloc("amax_g", [4, 128], F32)
    blast_g = sb.alloc("blast_g", [4, 128], F32)
    mshift = sb.alloc("mshift", [4, 136], F32)
    m127g = sb.alloc("m127g", [4, 128], F32)
    gsg = sb.alloc("gsg", [4, 128], F32)
    mprev = sb.alloc("mprev", [4, NB], F32)
    m127 = sb.alloc("m127", [4, NB], F32)
    nm127 = sb.alloc("nm127", [4, NB], F32)
    biasg = sb.alloc("biasg", [4, NB], F32)
    GIv = GI[:, :].rearrange("p (b k) -> p b k", k=128)
    gk = [(k, t) for t in range(NT) for k in ("gi", "gf")]
    fw.op("dve", lambda e: e.memset(ones4[:], 1.0), w=["ones4"])
    fw.op("dve", lambda e: e.tensor_scalar(out=GI[:, :], in0=GI[:, :], scalar1=15.0, scalar2=None, op0=ALU.mult),
          r=gk, w=["GI"])
    fw.op("act", lambda e: e.activation(out=sp_[:].rearrange("p b k -> p (b k)"), in_=GF[:, :], func=AF.Exp,
                                        scale=-15.0), r=gk, w=["sp"])
    fw.op("act", lambda e: e.activation(out=sp_[:].rearrange("p b k -> p (b k)"),
                                        in_=sp_[:].rearrange("p b k -> p (b k)"), func=AF.Ln,
                                        bias=cst[0:4, CL["one"]:CL["one"] + 1], scale=1.0), r=["sp", "cst"], w=["sp"])
    for j in range(NB):
        fw.op("dve", lambda e, j=j: e.tensor_tensor_scan(out=csp[:, j, :], data0=ones4[:], data1=sp_[:, j, :],
                                                         initial=0.0, op0=ALU.mult, op1=ALU.add),
              r=["sp", "ones4"], w=[("csp", j)])
    cspk = [("csp", j) for j in range(NB)]
    fw.op("dve", lambda e: e.tensor_tensor(out=a_[:], in0=GIv, in1=csp[:], op=ALU.add), r=["GI"] + cspk, w=["a"])
    for j in range(NB):
        fw.op("dve", lambda e, j=j: e.tensor_tensor_scan(out=cm[:, j, :], data0=a_[:, j, :], data1=a_[:, j, :],
                                                         initial=-1.0e30, op0=ALU.max, op1=ALU.max),
              r=["a"], w=[("cm", j)])
    cmk = [("cm", j) for j in range(NB)]
    fw.op("dve", lambda e: e.tensor_scalar(out=chs[:, :, 0:1], in0=csp[:, :, 127:128], scalar1=-1.0, scalar2=None,
                                           op0=ALU.mult), r=cspk, w=["chs0"])
    fw.op("dve", lambda e: e.tensor_copy(out=chs[:, :, 1:2], in_=cm[:, :, 127:128]), r=cmk, w=["chs1"])
    fw.dma("sp", CHS_loc, chs[:].rearrange("p a b -> p (a b)"), r=["chs0", "chs1"], w=["CHS_loc"], stream="io")
    fw.allgather(CHS_loc, CHS_all, r=["CHS_loc"], w=["CHS_all"])
    kv_ag()
    tl_all = sb.alloc("tl_all", [128, 8, 8, NB, 3], F32)
    halo = sb.alloc("halo", [128, 8, NB, 3], F32)
    fw.dma("sp", tl_all[:].rearrange("p r a b c -> p r (a b c)"), TL_all.rearrange("(r p) x -> p r x", p=128),
           r=["TL_all"], w=["tl_all"], stream="io")
    so = CL["sel"]
    fw.op("dve", lambda e: e.tensor_scalar(out=halo[:], in0=tl_all[:, 0], scalar1=cst[:, so:so + 1], scalar2=None,
                                           op0=ALU.mult), r=["tl_all", "cst"], w=["halo"])
    for r_ in range(1, 7):
        fw.op("dve", lambda e, r_=r_: e.scalar_tensor_tensor(out=halo[:], in0=tl_all[:, r_],
                                                             scalar=cst[:, so + r_:so + r_ + 1], in1=halo[:],
                                                             op0=ALU.mult, op1=ALU.add),
              r=["tl_all", "halo", "cst"], w=["halo"])
    if NB > 1:
        fw.op("dve", lambda e: e.scalar_tensor_tensor(out=halo[:, :, 1:NB, :], in0=tl_all[:, 7, :, 0:NB - 1, :],
                                                      scalar=cst[:, so + 7:so + 8], in1=halo[:, :, 1:NB, :],
                                                      op0=ALU.mult, op1=ALU.add),
              r=["tl_all", "halo", "cst"], w=["halo"])
    xp = [sb.alloc(f"xp{i}", [128, NB, 131], F32) for i in range(2)]
    yc = sb.alloc("yc", [128, NB, 128], F32)
    cw = CL["conv_w"]
    cb = CL["conv_b"]
    for ch in range(8):
        s = ch % 2
        fw.dma("sp", xp[s][:, :, 3:131], MQK[ch * 128:(ch + 1) * 128, :].rearrange("p (b k) -> p b k", k=128),
               r=[("mqk", t, ch) for t in range(NT)], w=[("xp", s)], stream="io")
        fw.op("dve", lambda e, s=s, ch=ch: e.tensor_copy(out=xp[s][:, :, 0:3], in_=halo[:, ch]),
              r=["halo", ("xp", s)], w=[("xp", s)])
        fw.op("dve", lambda e, s=s, ch=ch: e.tensor_scalar(out=yc[:], in0=xp[s][:, :, 0:128],
                                                           scalar1=cst[:, cw + ch * 4:cw + ch * 4 + 1],
                                                           scalar2=cst[:, cb + ch:cb + ch + 1], op0=ALU.mult,
                                                           op1=ALU.add), r=[("xp", s), "cst"], w=["yc"])
        for i in range(1, 4):
            fw.op("dve", lambda e, s=s, ch=ch, i=i: e.scalar_tensor_tensor(
                out=yc[:], in0=xp[s][:, :, i:i + 128], scalar=cst[:, cw + ch * 4 + i:cw + ch * 4 + i + 1], in1=yc[:],
                op0=ALU.mult, op1=ALU.add), r=[("xp", s), "yc", "cst"], w=["yc"])
        fw.op("act", lambda e, ch=ch: e.activation(out=qk_m[:, ch, :], in_=yc[:].rearrange("p b k -> p (b k)"),
                                                   func=AF.Silu), r=["yc"], w=[("qk_m", ch)])

    for r_ in range(8):
        fw.dma("sp", chs_all[:, :, r_, :], CHS_all[r_ * 4:(r_ + 1) * 4, :].rearrange("h (j x) -> h j x", x=2),
               r=["CHS_all"], w=[("chs_all", r_)], stream="io")
    cak = [("chs_all", r_) for r_ in range(8)]
    fw.op("dve", lambda e: e.tensor_copy(out=blast_g[:, 0:8 * NB].rearrange("p (j r) -> p j r", r=8), in_=chs_all[:, :, :, 0]),
          r=cak, w=["blast_g"])
    fw.op("dve", lambda e: e.tensor_copy(out=amax_g[:, 0:8 * NB].rearrange("p (j r) -> p j r", r=8), in_=chs_all[:, :, :, 1]),
          r=cak, w=["amax_g"])
    NCH = 8 * NB
    fw.op("dve", lambda e: e.memset(mshift[:], 0.0), w=["mshift"])
    fw.op("dve", lambda e: e.tensor_tensor_scan(out=mshift[:, 1:1 + NCH], data0=amax_g[:, 0:NCH],
                                                data1=blast_g[:, 0:NCH], initial=0.0, op0=ALU.max, op1=ALU.add),
          r=["amax_g", "blast_g", "mshift"], w=["mshift"])
    fw.op("dve", lambda e: e.tensor_tensor(out=m127g[:, 0:NCH], in0=mshift[:, 0:NCH], in1=amax_g[:, 0:NCH],
                                           op=ALU.max), r=["mshift", "amax_g"], w=["m127g"])
    fw.op("dve", lambda e: e.tensor_tensor(out=gsg[:, 0:NCH], in0=mshift[:, 0:NCH], in1=m127g[:, 0:NCH],
                                           op=ALU.subtract), r=["mshift", "m127g"], w=["gsg"])
    fw.op("act", lambda e: e.activation(out=gsg[:, 0:NCH], in_=gsg[:, 0:NCH], func=AF.Exp), r=["gsg"], w=["gsg"])
    for h in range(4):
        fw.op("pe", lambda e, h=h: e.matmul(banks[7][:, 0:NCH], cmat[0:4, 384 + h * 128:384 + (h + 1) * 128],
                                            gsg[:, 0:NCH], start=True, stop=True), r=["gsg", "cmat"], w=[pk(7)])
        fw.op("dve", lambda e, h=h: e.tensor_copy(out=gs_rep[:, h, 0:NCH], in_=banks[7][:, 0:NCH]), r=[pk(7)],
              w=[("gs_rep", h)])
    sc = CL["selc"]
    msv = mshift[:, 0:NCH].rearrange("p (j r) -> p j r", r=8)
    fw.op("dve", lambda e: e.tensor_scalar(out=mprev[:, :], in0=msv[:, :, 0], scalar1=cst[0:4, sc:sc + 1],
                                           scalar2=None, op0=ALU.mult), r=["mshift", "cst"], w=["mprev"])
    for r_ in range(1, 8):
        fw.op("dve", lambda e, r_=r_: e.scalar_tensor_tensor(out=mprev[:, :], in0=msv[:, :, r_],
                                                             scalar=cst[0:4, sc + r_:sc + r_ + 1], in1=mprev[:, :],
                                                             op0=ALU.mult, op1=ALU.add),
              r=["mshift", "mprev", "cst"], w=["mprev"])
    for j in range(NB):
        fw.op("dve", lambda e, j=j: e.tensor_scalar(out=Mt[:, j, :], in0=cm[:, j, :], scalar1=mprev[:, j:j + 1],
                                                    scalar2=None, op0=ALU.max), r=[("cm", j), "mprev"], w=[("M", j)])
    Mk = [("M", j) for j in range(NB)]
    fw.op("dve", lambda e: e.tensor_copy(out=m127[:, :], in_=Mt[:, :, 127]), r=Mk, w=["m127"])
    fw.op("dve", lambda e: e.tensor_scalar(out=nm127[:, :], in0=m127[:, :], scalar1=-1.0, scalar2=None, op0=ALU.mult),
          r=["m127"], w=["nm127"])
    fw.op("dve", lambda e: e.tensor_scalar(out=biasg[:, :], in0=mprev[:, :], scalar1=math.log(QSCALE), scalar2=None,
                                           op0=ALU.add), r=["mprev"], w=["biasg"])
    for j in range(NB):
        fw.op("act", lambda e, j=j: e.activation(out=wsq[:, j, :], in_=a_[:, j, :], func=AF.Exp,
                                                 bias=nm127[:, j:j + 1], scale=1.0), r=["a", "nm127"], w=[("q0", j)])
        fw.op("act", lambda e, j=j: e.activation(out=Eq[:, j, :], in_=Mt[:, j, :], func=AF.Exp,
                                                 bias=m127[:, j:j + 1], scale=-1.0), r=[("M", j), "m127"],
              w=[("q1", j)])
        fw.op("act", lambda e, j=j: e.activation(out=gq[:, j, :], in_=Mt[:, j, :], func=AF.Exp,
                                                 bias=biasg[:, j:j + 1], scale=-1.0), r=[("M", j), "biasg"],
              w=[("q2", j)])
    fw.op("dve", lambda e: e.tensor_tensor(out=clq[:], in0=csp[:], in1=Mt[:], op=ALU.subtract), r=cspk + Mk,
          w=["cl"])
    fw.op("act", lambda e: e.activation(out=clq[:].rearrange("p b k -> p (b k)"),
                                        in_=clq[:].rearrange("p b k -> p (b k)"), func=AF.Exp), r=["cl"], w=["cl"])
    for j in range(NB):
        for q, (T_, key) in enumerate(((wsq, ("q0", j)), (Eq, ("q1", j)), (gq, ("q2", j)), (clq, "cl"))):
            fw.op("pe", lambda e, j=j, q=q, T_=T_: e.transpose(banks[6][:, q * 4:(q + 1) * 4], T_[:, j, :],
                                                               ident_f[0:4, 0:4]),
                  r=[key, "cmat"], w=[("ps6", q)])
        fw.op("dve", lambda e, j=j: e.tensor_copy(out=cols_tok[:, j, :], in_=banks[6][:, 0:16]),
              r=[("ps6", q) for q in range(4)], w=[("cols", j)])

    fw.barrier(keep=kvkeys)
    sb.reset(emark)
    v_aug = sb.alloc("v_aug", [128, NB, 4, 257], BF16)
    kw = [sb.alloc(f"kw{i}", [128, 128], BF16) for i in range(2)]
    ust = [sb.alloc(f"ust{i}", [128, 4, 257], F32) for i in range(2)]
    psKT = banks[5].bitcast(BF16)

    def load_vaug():
        fw.op("dve", lambda e: e.memset(v_aug[:, :, :, 256:257], 1.0), w=["v_ones"])
        for j in range(NB):
            fw.dma("sp", v_aug[:, j, :, 0:256], MV[j * 128:(j + 1) * 128, :].rearrange("p (h e) -> p h e", e=256),
                   r=[("mv", j // 4, i, j % 4) for i in range(2)], w=[("v_aug", j)], stream="io")
    load_vaug()
    n = 0
    for j in range(NB):
        for h in range(4):
            s = n % 2
            n += 1
            fw.op("pe", lambda e, j=j, h=h: e.transpose(psKT[:, 0:128], qk_m[:, 4 + h, j * 128:(j + 1) * 128],
                                                        ident_b), r=[("qk_m", 4 + h), "cmatb"], w=[pk(5)])
            fw.op("dve", lambda e, j=j, h=h, s=s: e.tensor_scalar(out=kw[s][:, :], in0=psKT[:, 0:128],
                                                                  scalar1=cols_tok[:, j, h:h + 1], scalar2=None,
                                                                  op0=ALU.mult), r=[pk(5), ("cols", j)],
                  w=[("kw", s)])
            fw.op("pe", lambda e, j=j, h=h, s=s: e.matmul(banks[4][:, 0:257], kw[s][:, :], v_aug[:, j, h, :],
                                                          start=True, stop=True),
                  r=[("kw", s), ("v_aug", j), "v_ones"], w=[pk(4)])
            fw.op("act", lambda e, j=j, h=h: e.activation(out=ust[j % 2][:, h, :], in_=banks[4][:, 0:257],
                                                          func=AF.Copy), r=[pk(4)], w=[("ust", j % 2, h)])
        fw.dma("sp", U_loc[j * 512:(j + 1) * 512, :].rearrange("(h p) c -> p h c", p=128), ust[j % 2][:],
               r=[("ust", j % 2, h) for h in range(4)], w=[("U_loc", j)], stream="io")
    fw.allgather(U_loc, U_all, r=[("U_loc", j) for j in range(NB)], w=["U_all"])

    fw.barrier(keep=kvkeys + ["U_all"])
    sb.reset(emark)
    Kb = [sb.alloc(f"Kb{i}", [128, 8, NTOK], BF16) for i in range(2)]
    Vb = sb.alloc("Vb", [128, 8, NB, 256], BF16)
    o1n = sb.alloc("o1n", [128, NG, 2, 512], F32)
    accD = sb.alloc("accD", [128, 512], F32)
    accG = sb.alloc("accG", [128, 512], F32)
    top_ = sb.mark()
    if hole_hi - hole_lo >= 17 * 1024 + 512:
        sb.reset(hole_lo)
    qt = [sb.alloc(f"qt{i}", [128, 512], BF16) for i in range(2)]
    PT = [sb.alloc(f"PT{i}", [128, 512], BF16) for i in range(3)]
    Rl = sb.alloc("Rl", [128, 512], F32)
    dif = sb.alloc("dif", [128, 2, 512], F32)
    sqd = sb.alloc("sqd", [128, 2, 512], BF16)
    rsd = sb.alloc("rsd", [128, 512], F32)
    yst = [sb.alloc(f"yst{i}", [128, 512], BF16) for i in range(2)]
    gsc = sb.alloc("gsc", [128, 8], F32)
    if hole_hi - hole_lo >= 17 * 1024 + 512:
        assert sb.mark() <= hole_hi, (sb.mark(), hole_hi)
        sb.reset(top_)
    amask = cmatb[:, 256:256 + 1024].rearrange("p (r k) -> p r k", k=128)
    an = CL["ahn"]
    fw.op("dve", lambda e: e.tensor_scalar(out=gsc[:, :], in0=cst[:, an:an + 8], scalar1=1.0 - LAM_INIT, scalar2=None,
                                           op0=ALU.mult), r=["cst"], w=["gsc"])
    units = [(h, c) for h in range(4) for c in range(2)]

    def load_K(u):
        h, c = units[u]
        s = u % 2
        for r_ in range(8):
            row = r_ * 1024 + (h * 2 + c) * 128
            for t in range(NT):
                fw.dma("sp", Kb[s][:, r_, t * TT:(t + 1) * TT], KT_all[t][row:row + 128, :], r=[("KT_all", t)],
                       w=[("Kb", s, r_, t)], stream="kv")
    load_K(0)
    nq = 0
    npt = 0
    nys = 0
    for u, (h, c) in enumerate(units):
        if c == 0:
            for r_ in range(8):
                for t in range(NT):
                    fw.dma("sp", Vb[:, r_, t * 4:(t + 1) * 4, :],
                           V_all[t][r_ * TT:(r_ + 1) * TT, h * 256:(h + 1) * 256].rearrange("(j p) e -> p j e",
                                                                                             p=128),
                           r=[("V_all", t)], w=[("Vb", r_, t)], stream="kv")
        if u + 1 < len(units):
            load_K(u + 1)
        ks = u % 2
        for g in range(NG):
            qs = nq % 2
            nq += 1
            fw.dma("sp", qt[qs][:, :], QT[(h * 2 + c) * 128:(h * 2 + c + 1) * 128, g * 512:(g + 1) * 512],
                   r=[("aq", g, h * 2 + c)], w=[("qt", qs)], stream="io")
            nkv = 32 * g + 32

            def emitS(i, ks=ks, qs=qs, g=g):
                jlo = max(4 * g, i // 8)
                cs = (jlo - 4 * g) * 128
                r_, jp = i % 8, i // 8
                fw.op("pe", lambda e: e.matmul(banks[i % 2][:, cs:512], Kb[ks][:, r_, jp * 128:(jp + 1) * 128],
                                               qt[qs][:, cs:512], start=True, stop=True),
                      r=[("Kb", ks, r_, jp // 4), ("qt", qs)], w=[pk(i % 2)])
            emitS(0)
            for i in range(nkv):
                if i + 1 < nkv:
                    emitS(i + 1)
                jlo = max(4 * g, i // 8)
                cs = (jlo - 4 * g) * 128
                r_, jp = i % 8, i // 8
                ps_ = npt % 3
                npt += 1
                fw.op("act", lambda e, i=i, cs=cs, ps_=ps_: e.activation(out=PT[ps_][:, cs:512],
                                                                         in_=banks[i % 2][:, cs:512], func=AF.Exp,
                                                                         scale=QSCALE),
                      r=[pk(i % 2)], w=[("PT", ps_)])
                if i >= 32 * g:
                    fw.op("dve", lambda e, cs=cs, ps_=ps_, r_=r_: e.tensor_tensor(out=PT[ps_][:, cs:cs + 128],
                                                                                  in0=PT[ps_][:, cs:cs + 128],
                                                                                  in1=amask[:, r_, :], op=ALU.mult),
                          r=[("PT", ps_), "cmatb"], w=[("PT", ps_)])

                def av(e, i=i, cs=cs, ps_=ps_, r_=r_, jp=jp, nkv=nkv):
                    for ec in range(2):
                        ins = e.matmul(banks[2 + ec][:, cs:512], Vb[:, r_, jp, ec * 128:(ec + 1) * 128],
                                       PT[ps_][:, cs:512], start=(i == 0), stop=(i == nkv - 1))
                    return ins
                fw.op("pe", av, r=[("PT", ps_), ("Vb", r_, jp // 4)], w=[pk(2), pk(3)])
                aeng, acc, akey = (("dve", accD, "accD") if i % 2 == 0 else ("pool", accG, "accG"))
                if i < 2:
                    fw.op(aeng, lambda e, acc=acc, ps_=ps_: e.tensor_copy(out=acc[:, :], in_=PT[ps_][:, :]),
                          r=[("PT", ps_)], w=[akey])
                else:
                    fw.op(aeng, lambda e, acc=acc, ps_=ps_, cs=cs: e.tensor_tensor(out=acc[:, cs:512],
                                                                                  in0=acc[:, cs:512],
                                                                                  in1=PT[ps_][:, cs:512], op=ALU.add),
                          r=[("PT", ps_), akey], w=[akey])

            def lm(e):
                e.matmul(banks[4][:, :], ones_f, accD[:, :], start=True, stop=False)
                return e.matmul(banks[4][:, :], ones_f, accG[:, :], start=False, stop=True)
            fw.op("pe", lm, r=["accD", "accG", "cmat"], w=[pk(4)])
            fw.op("dve", lambda e: e.reciprocal(out=Rl[:, :], in_=banks[4][:, :]), r=[pk(4)], w=["Rl"])
            if c == 0:
                for ec in range(2):
                    fw.op("dve", lambda e, ec=ec, g=g: e.tensor_tensor(out=o1n[:, g, ec, :], in0=banks[2 + ec][:, :],
                                                                       in1=Rl[:, :], op=ALU.mult),
                          r=[pk(2 + ec), "Rl"], w=[("o1n", g, ec)])
            else:
                for ec in range(2):
                    fw.op("dve", lambda e, ec=ec: e.tensor_tensor(out=dif[:, ec, :], in0=banks[2 + ec][:, :],
                                                                  in1=Rl[:, :], op=ALU.mult),
                          r=[pk(2 + ec), "Rl"], w=[("dif", ec)])
                    fw.op("dve", lambda e, ec=ec, g=g: e.scalar_tensor_tensor(out=dif[:, ec, :], in0=dif[:, ec, :],
                                                                              scalar=neglam, in1=o1n[:, g, ec, :],
                                                                              op0=ALU.mult, op1=ALU.add),
                          r=[("dif", ec), ("o1n", g, ec), "lamt"], w=[("dif", ec)])
                rmsnorm_stats(dif, 512, rsd, sqd, 256, 7, [("dif", 0), ("dif", 1)], "a")
                for ec in range(2):
                    ys = nys % 2
                    nys += 1
                    fw.op("dve", lambda e, ec=ec, ys=ys, h=h: e.scalar_tensor_tensor(
                        out=yst[ys][:, :], in0=dif[:, ec, :], scalar=gsc[:, h * 2 + ec:h * 2 + ec + 1], in1=rsd[:, :],
                        op0=ALU.mult, op1=ALU.mult), r=[("dif", ec), "arstd", "gsc"], w=[("yst", ys)])
                    row = h * 256 + ec * 128
                    fw.dma("sp", YA[row:row + 128, g * 512:(g + 1) * 512], yst[ys][:, :], r=[("yst", ys)],
                           w=[("YA", g, h * 2 + ec)], stream="io")

    fw.barrier(keep=["U_all"])
    sb.reset(emark)
    v_aug = sb.alloc("v_aug2", [128, NB, 4, 257], BF16)
    csel = sb.alloc("csel", [128, NB, 4, 257], BF16)
    mo_sb = sb.alloc("mo_sb", [128, 8, NTOK], BF16)
    ur = [sb.alloc(f"ur{i}", [128, 4, 257], F32) for i in range(4)]
    crun = sb.alloc("crun", [128, 4, 257], F32)
    cacc = sb.alloc("cacc", [128, 4, 257], F32)
    load_vaug()
    for ch in range(8):
        fw.dma("sp", mo_sb[:, ch, :], MO[ch * 128:(ch + 1) * 128, :], r=[("mo", t, ch) for t in range(NT)],
               w=[("mo_sb", ch)], stream="io")
    Uv = U_all.rearrange("(r j h p) c -> r j p h c", r=8, j=NB, h=4, p=128)
    fw.op("dve", lambda e: e.memset(crun[:], 0.0), w=["crun"])
    for t_ in range(NCH):
        j, r_ = t_ // 8, t_ % 8
        us = t_ % 4
        fw.dma("sp", ur[us][:], Uv[r_, j], r=["U_all"], w=[("ur", us)], stream="u")
        if r_ == 0:
            fw.op("dve", lambda e: e.tensor_scalar(out=cacc[:], in0=crun[:], scalar1=cst[:, sc:sc + 1], scalar2=None,
                                                   op0=ALU.mult), r=["crun", "cst"], w=["cacc"])
        else:
            fw.op("dve", lambda e, r_=r_: e.scalar_tensor_tensor(out=cacc[:], in0=crun[:],
                                                                 scalar=cst[:, sc + r_:sc + r_ + 1], in1=cacc[:],
                                                                 op0=ALU.mult, op1=ALU.add),
                  r=["crun", "cacc", "cst"], w=["cacc"])
        if r_ == 7:
            fw.op("act", lambda e, j=j: e.activation(out=csel[:, j], in_=cacc[:], func=AF.Copy), r=["cacc"],
                  w=[("csel", j)])
        if t_ < NCH - 1:
            for h in range(4):
                fw.op("dve", lambda e, h=h, t_=t_, us=us: e.scalar_tensor_tensor(
                    out=crun[:, h, :], in0=crun[:, h, :], scalar=gs_rep[:, h, t_:t_ + 1], in1=ur[us][:, h, :],
                    op0=ALU.mult, op1=ALU.add), r=["crun", ("ur", us), ("gs_rep", h)], w=["crun"])

    P2 = [sb.alloc(f"P2{i}", [128, 128], BF16) for i in range(2)]
    tmpi = sb.alloc("tmpi", [128, 257], F32)
    tot = sb.alloc("tot", [128, 257], F32)
    sm = sb.alloc("sm", [128, 8], F32)
    junk = sb.alloc("junk", [128, 256], F32)
    hn = sb.alloc("hn", [128, 256], F32)
    ybs = [sb.alloc(f"ybs{i}", [128, 8, 128], BF16) for i in range(2)]
    mn = CL["mhn"]
    for j in range(NB):
        for h in range(4):
            ps_ = (j * 4 + h) % 2
            jc = slice(j * 128, (j + 1) * 128)
            fw.op("pe", lambda e, h=h, jc=jc: e.matmul(banks[0][:, 0:128], qk_m[:, 4 + h, jc], qk_m[:, h, jc],
                                                       start=True, stop=True), r=[("qk_m", h), ("qk_m", 4 + h)],
                  w=[pk(0)])
            fw.op("dve", lambda e, j=j, h=h, ps_=ps_: e.scalar_tensor_tensor(
                out=P2[ps_][:, :], in0=banks[0][:, 0:128], scalar=cols_tok[:, j, h:h + 1], in1=trimask,
                op0=ALU.mult, op1=ALU.mult), r=[pk(0), ("cols", j), "cmat"], w=[("P2", ps_)])
            fw.op("pe", lambda e, j=j, h=h, ps_=ps_: e.matmul(banks[1][:, 0:257], P2[ps_][:, :], v_aug[:, j, h, :],
                                                              start=True, stop=True),
                  r=[("P2", ps_), ("v_aug", j), "v_ones"], w=[pk(1)])
            fw.op("pe", lambda e, j=j, h=h, jc=jc: e.matmul(banks[2][:, 0:257], qk_m[:, h, jc], csel[:, j, h, :],
                                                            start=True, stop=True),
                  r=[("qk_m", h), ("csel", j)], w=[pk(2)])
            fw.op("dve", lambda e, j=j, h=h: e.tensor_scalar(out=tmpi[:, :], in0=banks[1][:, 0:257],
                                                             scalar1=cols_tok[:, j, 4 + h:5 + h], scalar2=None,
                                                             op0=ALU.mult), r=[pk(1), ("cols", j)], w=["tmpi"])
            fw.op("dve", lambda e, j=j, h=h: e.scalar_tensor_tensor(out=tot[:, :], in0=banks[2][:, 0:257],
                                                                    scalar=cols_tok[:, j, 8 + h:9 + h], in1=tmpi[:, :],
                                                                    op0=ALU.mult, op1=ALU.add),
                  r=[pk(2), ("cols", j), "tmpi"], w=["tot"])
            fw.op("act", lambda e: e.activation(out=sm[:, 6:7], in_=tot[:, 256:257], func=AF.Abs), r=["tot"],
                  w=["sm6"])
            fw.op("dve", lambda e, j=j, h=h: e.tensor_tensor(out=sm[:, 0:1], in0=sm[:, 6:7],
                                                             in1=cols_tok[:, j, 12 + h:13 + h], op=ALU.max),
                  r=["sm6", ("cols", j)], w=["sm0"])
            fw.op("dve", lambda e: e.reciprocal(out=sm[:, 1:2], in_=sm[:, 0:1]), r=["sm0"], w=["sm1"])
            fw.op("act", lambda e: e.activation(out=junk[:, :], in_=tot[:, 0:256], func=AF.Square, scale=sm[:, 1:2]),
                  r=["tot", "sm1"], w=["junk"])
            fw.op("dve", lambda e: e.tensor_reduce(out=sm[:, 2:3], in_=junk[:, :], axis=mybir.AxisListType.X,
                                                   op=ALU.add), r=["junk"], w=["sm2"])
            fw.op("act", lambda e: e.activation(out=sm[:, 3:4], in_=sm[:, 2:3], func=AF.Sqrt, bias=ccol("eps"),
                                                scale=1.0 / 256.0), r=["sm2", "cst"], w=["sm3"])
            fw.op("dve", lambda e: e.reciprocal(out=sm[:, 4:5], in_=sm[:, 3:4]), r=["sm3"], w=["sm4"])
            fw.op("dve", lambda e: e.tensor_tensor(out=sm[:, 5:6], in0=sm[:, 4:5], in1=sm[:, 1:2], op=ALU.mult),
                  r=["sm4", "sm1"], w=["sm5"])
            fw.op("dve", lambda e: e.tensor_scalar(out=hn[:, :], in0=tot[:, 0:256], scalar1=sm[:, 5:6], scalar2=None,
                                                   op0=ALU.mult), r=["tot", "sm5"], w=["hn"])
            for ec in range(2):
                fw.op("pe", lambda e, ec=ec: e.transpose(banks[3 + ec][:, 0:128], hn[:, ec * 128:(ec + 1) * 128],
                                                         ident_f), r=["hn", "cmat"], w=[pk(3 + ec)])
                fw.op("dve", lambda e, ec=ec, j=j, h=h, jc=jc: e.scalar_tensor_tensor(
                    out=ybs[j % 2][:, h * 2 + ec, :], in0=banks[3 + ec][:, 0:128],
                    scalar=cst[:, mn + h * 2 + ec:mn + h * 2 + ec + 1], in1=mo_sb[:, h * 2 + ec, jc], op0=ALU.mult,
                    op1=ALU.mult), r=[pk(3 + ec), ("mo_sb", h * 2 + ec), "cst"], w=[("ybs", j % 2, h * 2 + ec)])
        fw.dma("sp", YB[:, j * 128:(j + 1) * 128].rearrange("(c p) k -> p c k", p=128), ybs[j % 2][:],
               r=[("ybs", j % 2, q) for q in range(8)], w=[("YB", j)], stream="io")

    fw.barrier()
    sb.reset(pmark)
    b = alloc_ffn()
    yt = sb.alloc("yt", [128, 16, TT], BF16)
    ost = [sb.alloc(f"ost{i}", [128, TT], F32) for i in range(2)]
    plans = []
    for t in range(NT):
        kp = len(b.rg.tasks)
        for o in range(DC):
            def ld(slot, o=o):
                fw.dma("pool", b.wgu[slot][:, 0:16, 0, :], wout[:, o * 128:(o + 1) * 128].rearrange("(c p) n -> p c n", p=128),
                       w=[("wgu", slot, 0)], stream="wq")
            b.rg.add(ld)
        k0, k1 = plan_ffn(b, 1)
        plans.append((kp, k0, k1))
    for t in range(NT):
        kp, k0, k1 = plans[t]
        tc0 = t * TT
        for c in range(DC):
            fw.dma("sp", b.xt[:, c, :], X1[c * 128:(c + 1) * 128, tc0:tc0 + TT], r=[("X1", t, c)], w=[("x", c)],
                   stream="io")
        for q in range(8):
            fw.dma("sp", yt[:, q, :], YA[q * 128:(q + 1) * 128, tc0:tc0 + TT], r=[("YA", t, q)], w=[("yt", q)],
                   stream="io")
            fw.dma("sp", yt[:, 8 + q, :], YB[q * 128:(q + 1) * 128, tc0:tc0 + TT],
                   r=[("YB", j) for j in range(t * 4, t * 4 + 4)], w=[("yt", 8 + q)], stream="io")
        ytk = [("yt", q) for q in range(16)]
        for o in range(DC):
            slot = b.rg.use(kp + o)
            bank = 4 + o % 2

            def mm(e, bank=bank, slot=slot):
                for q in range(16):
                    ins = e.matmul(banks[bank][:, 0:TT], b.wgu[slot][:, q, 0, :], yt[:, q, :], start=(q == 0),
                                   stop=(q == 15))
                return ins
            fw.op("pe", mm, r=ytk + [("wgu", slot, 0)], w=[pk(bank)])
            fw.op("dve", lambda e, o=o, bank=bank: e.tensor_tensor(out=b.xt[:, o, :], in0=banks[bank][:, 0:TT],
                                                                   in1=b.xt[:, o, :], op=ALU.add),
                  r=[pk(bank), ("x", o)], w=[("x", o)])
        norm_to_h(b, "g_ffn2")
        run_ffn(b, k0, k1)
        xk = [("x", c) for c in range(DC)]
        rmsnorm_stats(b.xt, TT, b.rstd, b.sq, D, 6, xk, "n")
        for c in range(DC):
            s = c % 2
            fw.op("dve", lambda e, c=c, s=s: e.scalar_tensor_tensor(out=ost[s][:, :], in0=b.xt[:, c, :],
                                                                    scalar=ccol("g_final", c), in1=b.rstd[:, :],
                                                                    op0=ALU.mult, op1=ALU.mult),
                  r=[("x", c), "nrstd", "cst"], w=[("ost", s)])
            fw.dma("sp", outT[c * 128:(c + 1) * 128, tc0:tc0 + TT], ost[s][:, :], r=[("ost", s)], w=[("out", t, c)],
                   stream="io")
    fw.barrier()
    return nc, dbg


def host_prep(cfg, inp, c):
    S, D, DFF, NTOK, NB, DC = cfg.S, cfg.D, cfg.DFF, cfg.NTOK, cfg.NB, cfg.DC
    CL = cst_layout(DC)
    x = inp["x"][0]
    xc = x.reshape(NB, NCORES, 128, D)[:, c].reshape(NTOK, D)
    m = {"xT": np.ascontiguousarray(xc.T)}
    cst = np.zeros((128, CL["_n"]), np.float32)

    def pc(v):
        return np.asarray(v, np.float32).reshape(-1, 128).T
    cst[:, CL["g_ffn1"]:CL["g_ffn1"] + DC] = pc(inp["ffn1_norm"][0])
    cst[:, CL["g_mix"]:CL["g_mix"] + DC] = pc(inp["mix_norm"][0])
    cst[:, CL["g_ffn2"]:CL["g_ffn2"] + DC] = pc(inp["ffn2_norm"][0])
    cst[:, CL["g_final"]:CL["g_final"] + DC] = pc(inp["final_norm"])
    cw = np.asarray(inp["conv_w"][0][:, 0, :], np.float32)
    cst[:, CL["conv_w"]:CL["conv_w"] + 32] = cw.reshape(4, 8, 128).transpose(2, 1, 0).reshape(128, 32)
    cst[:, CL["conv_b"]:CL["conv_b"] + 8] = pc(inp["conv_b"][0])
    cst[:, CL["ahn"]:CL["ahn"] + 8] = pc(inp["attn_head_norm"][0].reshape(-1))
    cst[:, CL["mhn"]:CL["mhn"] + 8] = pc(inp["mlstm_head_norm"][0].reshape(-1))
    for i, k in enumerate(["lam_q1", "lam_k1", "lam_q2", "lam_k2"]):
        cst[:, CL["lam"] + i] = inp[k][0]
    sel = np.zeros(8, np.float32)
    if c == 0:
        sel[7] = 1.0
    else:
        sel[c - 1] = 1.0
    cst[:, CL["sel"]:CL["sel"] + 8] = sel[None, :]
    selc = np.zeros(8, np.float32)
    selc[c] = 1.0
    cst[:, CL["selc"]:CL["selc"] + 8] = selc[None, :]
    cst[0:4, CL["bgate"]] = inp["b_igate"][0]
    cst[0:4, CL["bgate"] + 1] = inp["b_fgate"][0]
    cst[:, CL["eps"]] = EPS
    cst[:, CL["one"]] = 1.0
    m["cst"] = cst
    cmat = np.zeros((128, 3 * 128 + 512), np.float32)
    cmat[:, 0:128] = np.eye(128, dtype=np.float32)
    cmat[:, 128:256] = np.triu(np.ones((128, 128), np.float32)) * QSCALE
    cmat[:, 256:384] = 1.0
    for h in range(4):
        cmat[h, 384 + h * 128:384 + (h + 1) * 128] = 1.0
    m["cmat"] = cmat
    cmb = np.zeros((128, 10 * 128), np.float32)
    cmb[:, 0:128] = np.eye(128, dtype=np.float32)
    cmb[:, 128:256] = 1.0
    for r in range(8):
        if r < c:
            cmb[:, 256 + r * 128:256 + (r + 1) * 128] = 1.0
        elif r == c:
            cmb[:, 256 + r * 128:256 + (r + 1) * 128] = np.triu(np.ones((128, 128), np.float32))
    m["cmatb"] = cmb
    return m


_CACHE = {}


def run(cfg, inp, debug=False):
    key = (cfg.S, cfg.D, cfg.DFF, debug)
    if key not in _CACHE:
        _CACHE[key] = build(cfg, debug)
    nc, dbg = _CACHE[key]
    shared = {
        "wg1": np.ascontiguousarray(inp["ffn1_w_gate"][0]), "wu1": np.ascontiguousarray(inp["ffn1_w_up"][0]),
        "wd1": np.ascontiguousarray(inp["ffn1_w_down"][0]), "wg2": np.ascontiguousarray(inp["ffn2_w_gate"][0]),
        "wu2": np.ascontiguousarray(inp["ffn2_w_up"][0]), "wd2": np.ascontiguousarray(inp["ffn2_w_down"][0]),
        "win": np.ascontiguousarray(inp["w_in"][0]), "wout": np.ascontiguousarray(inp["w_out"][0]),
    }
    in_maps = []
    for c in range(NCORES):
        m = host_prep(cfg, inp, c)
        m.update(shared)
        in_maps.append(m)
    res = run_bass_kernel_spmd(nc, in_maps, core_ids=list(range(NCORES)))
    NB, NTOK, D = cfg.NB, cfg.NTOK, cfg.D
    out = np.empty((NB, NCORES, 128, D), np.float32)
    for c in range(NCORES):
        out[:, c] = res.results[c]["outT"].T.reshape(NB, 128, D)
    return out.reshape(1, cfg.S, D), res


def kernel(**inputs):
    inp = {k: np.asarray(v) for k, v in inputs.items()}
    cfg = Cfg(inp["x"].shape[1], inp["x"].shape[2], inp["ffn1_w_gate"].shape[2])
    out, _ = run(cfg, inp)
    return out
```

```python
import math
import numpy as np
import concourse.bass as bass
import concourse.mybir as mybir
from concourse.bass_utils import run_bass_kernel_spmd

F32 = mybir.dt.float32
BF16 = mybir.dt.bfloat16
AF = mybir.ActivationFunctionType
ALU = mybir.AluOpType

NCORES = 8
N_IN = 6152
EPS = 1e-6
SB_LO = 16512
SB_HI = 229344
QSCALE = 128 ** -0.5
LAM_INIT = 0.8 - 0.6 * math.exp(0.0)

def cst_layout(DC):
    o = {}
    p = 0
    for nm, n in [("g_ffn1", DC), ("g_mix", DC), ("g_ffn2", DC), ("g_final", DC), ("conv_w", 32), ("conv_b", 8),
                  ("ahn", 8), ("mhn", 8), ("lam", 4), ("sel", 8), ("selc", 8), ("bgate", 2), ("eps", 1), ("one", 1),
                  ("lng", 1)]:
        o[nm] = p
        p += n
    o["_n"] = p
    return o


class Cfg:
    def __init__(self, S, D, DFF):
        self.S, self.D, self.DFF = S, D, DFF
        self.NTOK = S // NCORES
        self.NB = self.NTOK // 128
        self.TT = 512
        self.NT = self.NTOK // self.TT
        self.DC = D // 128
        self.FC = DFF // 128
        self.NG = self.NB // 4
        assert self.NTOK % 512 == 0 and D % 128 == 0 and DFF % 128 == 0


class EngS:
    def __init__(self, name, eng, sem, semi):
        self.name, self.eng, self.sem, self.semi = name, eng, sem, semi
        self.count = 0
        self.waited = {}


class FW:
    def __init__(self, nc):
        self.nc = nc
        self.sems = []
        self.engs = {}
        for nm, e in [("pe", nc.tensor), ("act", nc.scalar), ("dve", nc.vector), ("pool", nc.gpsimd), ("sp", nc.sync)]:
            s = self.newsem("e_" + nm)
            self.engs[nm] = EngS(nm, e, self.sems[s], s)
        self.regions = {}
        self.streams = {}
        self.lat = {}
        self.ccsems = set()

    def newsem(self, name):
        self.sems.append(self.nc.alloc_semaphore(name))
        return len(self.sems) - 1

    def _wait(self, E, tok):
        si, val = tok
        if E.waited.get(si, 0) >= val:
            return
        E.eng.wait_ge(self.sems[si], val)
        E.waited[si] = val

    def _deps(self, E, r, w):
        need = {}
        for k in r:
            reg = self.regions.get(k)
            if reg:
                for si, v in reg[0].items():
                    if need.get(si, 0) < v:
                        need[si] = v
        for k in w:
            reg = self.regions.get(k)
            if reg:
                for d in reg:
                    for si, v in d.items():
                        if need.get(si, 0) < v:
                            need[si] = v
        for si, v in need.items():
            if si == E.semi and E.name == "pe":
                continue
            self._wait(E, (si, v))

    def _upd(self, r, w, tok):
        si, v = tok
        self.lat[si] = max(self.lat.get(si, 0), v)
        for k in r:
            reg = self.regions.setdefault(k, [{}, {}])
            if reg[1].get(si, 0) < v:
                reg[1][si] = v
        for k in w:
            self.regions[k] = [{si: v}, {}]

    def op(self, e, fn, r=(), w=()):
        E = self.engs[e]
        self._deps(E, r, w)
        inst = fn(E.eng)
        E.count += 1
        inst.then_inc(E.sem, 1)
        self._upd(r, w, (E.semi, E.count))

    def dma(self, q, out, in_, r=(), w=(), stream="d", R=8):
        Q = self.engs[q]
        self._deps(Q, r, w)
        st = self.streams.get(stream)
        if st is None:
            st = {"sems": [self.newsem(f"s_{stream}_{i}") for i in range(R)], "n": 0}
            self.streams[stream] = st
        R = len(st["sems"])
        i = st["n"]
        st["n"] += 1
        si = st["sems"][i % R]
        if i >= R:
            self._wait(Q, (si, 16 * (i // R)))
        Q.eng.dma_start(out=out, in_=in_).then_inc(self.sems[si], 16)
        self._upd(r, w, (si, 16 * (i // R + 1)))

    def allgather(self, ins, outs, r=(), w=(), qos=None):
        Q = self.engs["pool"]
        self._deps(Q, r, w)
        si = self.newsem(f"cc{len(self.sems)}")
        self.ccsems.add(si)
        Q.eng.collective_compute("AllGather", ALU.bypass, replica_groups=[list(range(NCORES))],
                                 ins=[ins.opt()], outs=[outs.opt()], dma_qos=qos).then_inc(self.sems[si])
        self._upd(r, w, (si, 1))

    def barrier(self, keep=()):
        saved = {k: self.regions[k] for k in keep if k in self.regions}
        skip = set()
        for reg in saved.values():
            for d in reg:
                skip.update(k for k in d.keys() if k in self.ccsems)
        for E in self.engs.values():
            for si, v in self.lat.items():
                if si == E.semi or si in skip:
                    continue
                self._wait(E, (si, v))
        self.regions = dict(saved)


class Ring:
    def __init__(self, fw, name, nslots):
        self.fw, self.name, self.n = fw, name, nslots
        self.tasks = []
        self.issued = 0

    def add(self, fn):
        self.tasks.append(fn)
        return len(self.tasks) - 1

    def prefetch(self, upto):
        upto = min(upto, len(self.tasks) - 1)
        while self.issued <= upto:
            k = self.issued
            self.tasks[k](k % self.n)
            self.issued += 1

    def use(self, k, limit=None):
        up = k + self.n - 1
        if limit is not None:
            up = min(up, limit)
        self.prefetch(up)
        return k % self.n

    def key(self, slot):
        return (self.name, slot)


class SB:
    def __init__(self, nc):
        self.nc = nc
        self.p = SB_LO
        self.i = 0

    def alloc(self, name, shape, dt):
        n = 1
        for s in shape[1:]:
            n *= s
        nbytes = n * (4 if dt == F32 else 2)
        off = (self.p + 63) // 64 * 64
        assert off + nbytes <= SB_HI, f"SBUF overflow at {name}: {off + nbytes}"
        self.p = off + nbytes
        self.i += 1
        return self.nc.alloc_sbuf_tensor_at(f"{name}_{self.i}", list(shape), dt, offset=off)

    def mark(self):
        return self.p

    def reset(self, m):
        self.p = m


def build(cfg, debug=False):
    S, D, DFF, NTOK, NB, TT, NT, DC, FC, NG = (cfg.S, cfg.D, cfg.DFF, cfg.NTOK, cfg.NB, cfg.TT, cfg.NT, cfg.DC,
                                                cfg.FC, cfg.NG)
    nc = bass.Bass("TRN2", target_bir_lowering=False)
    fw = FW(nc)
    sb = SB(nc)
    CL = cst_layout(DC)

    def din(name, shape, dt=F32):
        return nc.dram_tensor(name, list(shape), dt, kind="ExternalInput").ap()

    xT = din("xT", [D, NTOK])
    wg = [din("wg1", [D, DFF]), din("wg2", [D, DFF])]
    wu = [din("wu1", [D, DFF]), din("wu2", [D, DFF])]
    wd = [din("wd1", [DFF, D]), din("wd2", [DFF, D])]
    win = din("win", [D, N_IN])
    wout = din("wout", [2048, D])
    cst_d = din("cst", [128, CL["_n"]])
    cmat_d = din("cmat", [128, 3 * 128 + 512])
    cmatb_d = din("cmatb", [128, 2 * 128 + 8 * 128])
    outT = nc.dram_tensor("outT", [D, NTOK], F32, kind="ExternalOutput").ap()

    dbg = {}

    def scratch(name, shape, dt, cc=False):
        if debug and not cc:
            t = nc.dram_tensor(name, list(shape), dt, kind="ExternalOutput")
            dbg[name] = True
        else:
            t = nc.dram_tensor(name, list(shape), dt)
        return t.ap()

    X1 = scratch("X1", [D, NTOK], F32)
    QT = scratch("QT", [1024, NTOK], BF16)
    KT_loc = [scratch(f"KT_loc{t}", [1024, TT], BF16, cc=True) for t in range(NT)]
    KT_all = [scratch(f"KT_all{t}", [8 * 1024, TT], BF16, cc=True) for t in range(NT)]
    V_loc = [scratch(f"V_loc{t}", [TT, 1024], BF16, cc=True) for t in range(NT)]
    V_all = [scratch(f"V_all{t}", [8 * TT, 1024], BF16, cc=True) for t in range(NT)]
    kvkeys = [("KT_all", t) for t in range(NT)] + [("V_all", t) for t in range(NT)]
    MQK = scratch("MQK", [1024, NTOK], F32)
    MV = scratch("MV", [NTOK, 1024], BF16)
    MO = scratch("MO", [1024, NTOK], BF16)
    TL_loc = scratch("TL_loc", [128, 8 * NB * 3], F32, cc=True)
    TL_all = scratch("TL_all", [8 * 128, 8 * NB * 3], F32, cc=True)
    CHS_loc = scratch("CHS_loc", [4, NB * 2], F32, cc=True)
    CHS_all = scratch("CHS_all", [32, NB * 2], F32, cc=True)
    U_loc = scratch("U_loc", [NB * 4 * 128, 257], F32, cc=True)
    U_all = scratch("U_all", [8 * NB * 4 * 128, 257], F32, cc=True)
    YA = scratch("YA", [1024, NTOK], BF16)
    YB = scratch("YB", [1024, NTOK], BF16)

    banks = [nc.alloc_psum_tensor(f"bank{i}", [128, 512], F32) for i in range(8)]

    def pk(i):
        return ("ps", i)

    cst = sb.alloc("cst", [128, CL["_n"]], F32)
    cmat = sb.alloc("cmat", [128, 3 * 128 + 512], F32)
    cmatb = sb.alloc("cmatb", [128, 10 * 128], BF16)
    lamt = sb.alloc("lamt", [128, 8], F32)
    hole_lo = sb.mark()
    GI = sb.alloc("GI", [4, NTOK], F32)
    GF = sb.alloc("GF", [4, NTOK], F32)
    tails_sb = sb.alloc("tails", [128, 8, NB, 3], F32)
    hole_hi = sb.mark()
    fw.dma("sp", cst[:], cst_d, w=["cst"], stream="io")
    fw.dma("sp", cmat[:], cmat_d, w=["cmat"], stream="io")
    fw.dma("pool", cmatb[:], cmatb_d, w=["cmatb"], stream="wq")
    ident_f = cmat[:, 0:128]
    trimask = cmat[:, 128:256]
    ones_f = cmat[:, 256:384]
    ident_b = cmatb[:, 0:128]
    ones_b = cmatb[:, 128:256]

    def ccol(name, i=0):
        return cst[:, CL[name] + i:CL[name] + i + 1]

    lo = CL["lam"]
    fw.op("dve", lambda e: e.tensor_tensor(out=lamt[:, 0:1], in0=cst[:, lo:lo + 1], in1=cst[:, lo + 1:lo + 2],
                                           op=ALU.mult), r=["cst"], w=["lamt"])
    fw.op("dve", lambda e: e.tensor_tensor(out=lamt[:, 1:2], in0=cst[:, lo + 2:lo + 3], in1=cst[:, lo + 3:lo + 4],
                                           op=ALU.mult), r=["cst", "lamt"], w=["lamt"])
    fw.op("pe", lambda e: e.matmul(banks[7][:, 0:2], ones_f, lamt[:, 0:2], start=True, stop=True),
          r=["cmat", "lamt"], w=[pk(7)])
    fw.op("act", lambda e: e.activation(out=lamt[:, 2:4], in_=banks[7][:, 0:2], func=AF.Exp), r=[pk(7), "lamt"],
          w=["lamt"])
    fw.op("dve", lambda e: e.tensor_tensor(out=lamt[:, 5:6], in0=lamt[:, 3:4], in1=lamt[:, 2:3], op=ALU.subtract),
          r=["lamt"], w=["lamt"])
    fw.op("dve", lambda e: e.tensor_scalar(out=lamt[:, 4:5], in0=lamt[:, 5:6], scalar1=-LAM_INIT, scalar2=None,
                                           op0=ALU.add), r=["lamt"], w=["lamt"])
    neglam = lamt[:, 4:5]
    pmark = sb.mark()

    def rmsnorm_stats(xt, T, rstd, sq, nfeat, bank, xkeys, tag):
        nch = len(xkeys)
        for c in range(nch):
            fw.op("act", lambda e, c=c: e.activation(out=sq[:, c % 2, :], in_=xt[:, c, :], func=AF.Square),
                  r=[xkeys[c]], w=[(tag + "sq", c % 2)])
            fw.op("pe", lambda e, c=c: e.matmul(banks[bank][:, 0:T], ones_b, sq[:, c % 2, :], start=(c == 0),
                                                stop=(c == nch - 1)),
                  r=[(tag + "sq", c % 2), "cmatb"], w=[pk(bank)])
        fw.op("act", lambda e: e.activation(out=rstd[:, 0:T], in_=banks[bank][:, 0:T], func=AF.Sqrt,
                                            bias=ccol("eps"), scale=1.0 / nfeat),
              r=[pk(bank), "cst"], w=[tag + "rstd"])
        fw.op("dve", lambda e: e.reciprocal(out=rstd[:, 0:T], in_=rstd[:, 0:T]), r=[tag + "rstd"], w=[tag + "rstd"])

    class FFNBufs:
        pass

    def alloc_ffn():
        b = FFNBufs()
        b.xt = sb.alloc("xt", [128, DC, TT], F32)
        b.h = sb.alloc("h", [128, DC, TT], BF16)
        b.act = sb.alloc("act", [128, FC, TT], BF16)
        b.sq = sb.alloc("sq", [128, 2, TT], BF16)
        b.rstd = sb.alloc("rstd", [128, TT], F32)
        b.sg = sb.alloc("sg", [128, 2, TT], F32)
        b.wgu = [sb.alloc(f"wgu{i}", [128, max(DC, 16), 2, 128], BF16) for i in range(3)]
        b.wd = [sb.alloc(f"wd{i}", [128, FC, 128], BF16) for i in range(3)]
        b.rg = Ring(fw, "wgu", 3)
        b.rd = Ring(fw, "wd", 3)
        return b

    def norm_to_h(b, gname):
        xk = [("x", c) for c in range(DC)]
        rmsnorm_stats(b.xt, TT, b.rstd, b.sq, D, 6, xk, "n")
        for c in range(DC):
            fw.op("dve", lambda e, c=c: e.scalar_tensor_tensor(out=b.h[:, c, :], in0=b.xt[:, c, :],
                                                               scalar=ccol(gname, c), in1=b.rstd[:, :],
                                                               op0=ALU.mult, op1=ALU.mult),
                  r=[("x", c), "nrstd", "cst"], w=[("h", c)])

    def plan_ffn(b, l):
        k0 = len(b.rg.tasks)
        for f in range(FC):
            def ld(slot, f=f):
                fw.dma("pool", b.wgu[slot][:, 0:DC, 0, :],
                       wg[l][:, f * 128:(f + 1) * 128].rearrange("(c p) n -> p c n", p=128),
                       w=[("wgu", slot, 0)], stream="wq")
                fw.dma("pool", b.wgu[slot][:, 0:DC, 1, :],
                       wu[l][:, f * 128:(f + 1) * 128].rearrange("(c p) n -> p c n", p=128),
                       w=[("wgu", slot, 1)], stream="wq")
            b.rg.add(ld)
        k1 = len(b.rd.tasks)
        for o in range(DC):
            def ld(slot, o=o):
                fw.dma("pool", b.wd[slot][:, :, :],
                       wd[l][:, o * 128:(o + 1) * 128].rearrange("(c p) n -> p c n", p=128),
                       w=[("wd", slot)], stream="wq")
            b.rd.add(ld)
        return k0, k1

    def run_ffn(b, k0, k1):
        hk = [("h", c) for c in range(DC)]
        b.rd.prefetch(k1)
        for f in range(FC):
            slot = b.rg.use(k0 + f)
            gb, ub = f % 2, 2 + f % 2
            for (bank, j) in ((gb, 0), (ub, 1)):
                def mm(e, bank=bank, j=j, slot=slot):
                    for c in range(DC):
                        ins = e.matmul(banks[bank][:, 0:TT], b.wgu[slot][:, c, j, :], b.h[:, c, :], start=(c == 0),
                                       stop=(c == DC - 1))
                    return ins
                fw.op("pe", mm, r=hk + [("wgu", slot, j)], w=[pk(bank)])
            fw.op("act", lambda e, f=f, gb=gb: e.activation(out=b.sg[:, f % 2, :], in_=banks[gb][:, 0:TT],
                                                            func=AF.Silu), r=[pk(gb)], w=[("sg", f % 2)])
            fw.op("dve", lambda e, f=f, ub=ub: e.tensor_tensor(out=b.act[:, f, :], in0=b.sg[:, f % 2, :],
                                                               in1=banks[ub][:, 0:TT], op=ALU.mult),
                  r=[("sg", f % 2), pk(ub)], w=[("act", f)])
        ak = [("act", f) for f in range(FC)]
        for o in range(DC):
            slot = b.rd.use(k1 + o)
            bank = 4 + o % 2

            def mm(e, bank=bank, slot=slot):
                for f in range(FC):
                    ins = e.matmul(banks[bank][:, 0:TT], b.wd[slot][:, f, :], b.act[:, f, :], start=(f == 0),
                                   stop=(f == FC - 1))
                return ins
            fw.op("pe", mm, r=ak + [("wd", slot)], w=[pk(bank)])
            fw.op("dve", lambda e, o=o, bank=bank: e.scalar_tensor_tensor(out=b.xt[:, o, :], in0=banks[bank][:, 0:TT],
                                                                          scalar=0.5, in1=b.xt[:, o, :], op0=ALU.mult,
                                                                          op1=ALU.add),
                  r=[pk(bank), ("x", o)], w=[("x", o)])

    b = alloc_ffn()
    stb = [sb.alloc(f"stb{i}", [128, TT], BF16) for i in range(3)]
    stf = [sb.alloc(f"stf{i}", [128, TT], F32) for i in range(2)]
    if FC >= 32 and DC <= 16:
        wtm = [b.act[:, 16 * i:16 * (i + 1), :] for i in range(2)]
        wtm_al = [[("act", f) for f in range(16 * i, 16 * i + 16)] for i in range(2)]
    else:
        wtm = [sb.alloc(f"wtm{i}", [128, DC, 512], BF16)[:] for i in range(2)]
        wtm_al = [[], []]
    rt = Ring(fw, "wtm", 2)

    fm = []
    for i in range(8):
        fm.append((i * 128, 128, "aq", i))
    for i in range(8):
        fm.append((1024 + i * 128, 128, "ak", i))
    for i in range(8):
        fm.append((3072 + i * 128, 128, "mqk", i))
    for i in range(8):
        fm.append((5120 + i * 128, 128, "mo", i))
    fm.append((6144, 4, "gi", 0))
    fm.append((6148, 4, "gf", 0))
    tmg = [(2048, "av", 0), (2560, "av", 1), (4096, "mv", 0), (4608, "mv", 1)]

    plans = []
    for t in range(NT):
        k0, k1 = plan_ffn(b, 0)
        kp = len(b.rg.tasks)
        for (c0, M, kind, idx) in fm:
            def ld(slot, c0=c0, M=M):
                fw.dma("pool", b.wgu[slot][:, 0:DC, 0, 0:M], win[:, c0:c0 + M].rearrange("(c p) n -> p c n", p=128),
                       w=[("wgu", slot, 0)], stream="wq")
            b.rg.add(ld)
        kt = len(rt.tasks)
        for (c0, kind, idx) in tmg:
            def ld(slot, c0=c0):
                fw.dma("pool", wtm[slot][:, 0:DC, :], win[:, c0:c0 + 512].rearrange("(c p) n -> p c n", p=128),
                       w=[("wtm", slot)] + wtm_al[slot], stream="wq")
            rt.add(ld)
        plans.append((k0, k1, kp, kt))

    nstb = 0
    nstf = 0
    for t in range(NT):
        k0, k1, kp, kt = plans[t]
        tc0 = t * TT
        for c in range(DC):
            fw.dma("sp", b.xt[:, c, :], xT[c * 128:(c + 1) * 128, tc0:tc0 + TT], w=[("x", c)], stream="io")
        norm_to_h(b, "g_ffn1")
        run_ffn(b, k0, k1)
        rt.prefetch(kt)
        for c in range(DC):
            fw.dma("sp", X1[c * 128:(c + 1) * 128, tc0:tc0 + TT], b.xt[:, c, :], r=[("x", c)], w=[("X1", t, c)],
                   stream="io")
        norm_to_h(b, "g_mix")
        hk = [("h", c) for c in range(DC)]
        for q, (c0, M, kind, idx) in enumerate(fm):
            slot = b.rg.use(kp + q)
            bank = 4 + q % 2

            def mm(e, bank=bank, slot=slot, M=M):
                for c in range(DC):
                    ins = e.matmul(banks[bank][0:M, 0:TT], b.wgu[slot][:, c, 0, 0:M], b.h[:, c, :], start=(c == 0),
                                   stop=(c == DC - 1))
                return ins
            fw.op("pe", mm, r=hk + [("wgu", slot, 0)], w=[pk(bank)])
            if kind in ("aq", "ak", "mo"):
                s = nstb % 3
                nstb += 1
                if kind == "mo":
                    fw.op("act", lambda e, s=s, bank=bank: e.activation(out=stb[s][:, :], in_=banks[bank][:, 0:TT],
                                                                        func=AF.Sigmoid), r=[pk(bank)], w=[("stb", s)])
                elif kind == "aq":
                    fw.op("dve", lambda e, s=s, bank=bank: e.tensor_copy(out=stb[s][:, :], in_=banks[bank][:, 0:TT]),
                          r=[pk(bank)], w=[("stb", s)])
                else:
                    fw.op("act", lambda e, s=s, bank=bank: e.activation(out=stb[s][:, :], in_=banks[bank][:, 0:TT],
                                                                        func=AF.Copy), r=[pk(bank)], w=[("stb", s)])
                if kind == "ak":
                    dap = KT_loc[t][idx * 128:(idx + 1) * 128, :]
                else:
                    dap = {"aq": QT, "mo": MO}[kind][idx * 128:(idx + 1) * 128, tc0:tc0 + TT]
                fw.dma("sp", dap, stb[s][:, :], r=[("stb", s)], w=[(kind, t, idx)], stream="io")
            elif kind == "mqk":
                s = nstf % 2
                nstf += 1
                fw.op("dve", lambda e, s=s, bank=bank: e.tensor_copy(out=stf[s][:, :], in_=banks[bank][:, 0:TT]),
                      r=[pk(bank)], w=[("stf", s)])
                fw.op("dve", lambda e, s=s, idx=idx, t=t: e.tensor_copy(
                    out=tails_sb[:, idx, t * 4:(t + 1) * 4, :],
                    in_=stf[s][:, :].rearrange("p (b k) -> p b k", k=128)[:, :, 125:128]),
                    r=[("stf", s)], w=[("tails", t, idx)])
                fw.dma("sp", MQK[idx * 128:(idx + 1) * 128, tc0:tc0 + TT], stf[s][:, :], r=[("stf", s)],
                       w=[("mqk", t, idx)], stream="io")
            else:
                G = GI if kind == "gi" else GF
                bc = 0 if kind == "gi" else 1
                fw.op("dve", lambda e, bank=bank, G=G, bc=bc: e.tensor_scalar(
                    out=G[:, tc0:tc0 + TT], in0=banks[bank][0:4, 0:TT], scalar1=cst[0:4, CL["bgate"] + bc:CL["bgate"] + bc + 1],
                    scalar2=1.0 / 15.0, op0=ALU.add, op1=ALU.mult), r=[pk(bank), "cst"], w=[(kind, t)])
                fw.op("act", lambda e, G=G: e.activation(out=G[:, tc0:tc0 + TT], in_=G[:, tc0:tc0 + TT], func=AF.Tanh),
                      r=[(kind, t)], w=[(kind, t)])
        for q, (c0, kind, idx) in enumerate(tmg):
            slot = rt.use(kt + q, limit=kt + len(tmg) - 1)
            for blk in range(4):
                bank = 4 + (q * 4 + blk) % 2

                def mm(e, bank=bank, slot=slot, blk=blk):
                    for c in range(DC):
                        ins = e.matmul(banks[bank][:, 0:512], b.h[:, c, blk * 128:(blk + 1) * 128], wtm[slot][:, c, :],
                                       start=(c == 0), stop=(c == DC - 1))
                    return ins
                fw.op("pe", mm, r=hk + [("wtm", slot)] + wtm_al[slot], w=[pk(bank)])
                s = nstb % 3
                nstb += 1
                if blk % 2 == 0:
                    fw.op("act", lambda e, s=s, bank=bank: e.activation(out=stb[s][:, :], in_=banks[bank][:, 0:512],
                                                                        func=AF.Copy), r=[pk(bank)], w=[("stb", s)])
                else:
                    fw.op("dve", lambda e, s=s, bank=bank: e.tensor_copy(out=stb[s][:, :], in_=banks[bank][:, 0:512]),
                          r=[pk(bank)], w=[("stb", s)])
                if kind == "av":
                    dap = V_loc[t][blk * 128:(blk + 1) * 128, idx * 512:(idx + 1) * 512]
                else:
                    r0 = tc0 + blk * 128
                    dap = MV[r0:r0 + 128, idx * 512:(idx + 1) * 512]
                fw.dma("sp", dap, stb[s][:, :], r=[("stb", s)], w=[(kind, t, idx, blk)], stream="io")

        def kv_ag(t=t):
            fw.allgather(KT_loc[t], KT_all[t], r=[("ak", t, i) for i in range(8)], w=[("KT_all", t)])
            fw.allgather(V_loc[t], V_all[t], r=[("av", t, i, bl) for i in range(2) for bl in range(4)],
                         w=[("V_all", t)])
        if t < NT - 1:
            kv_ag()

    fw.dma("sp", TL_loc, tails_sb[:].rearrange("p a b c -> p (a b c)"),
           r=[("tails", t, i) for t in range(NT) for i in range(8)], w=["TL_loc"], stream="io")
    fw.allgather(TL_loc, TL_all, r=["TL_loc"], w=["TL_all"])

    lastkv = [("ak", NT - 1, i) for i in range(8)] + [("av", NT - 1, i, bl) for i in range(2) for bl in range(4)]
    fw.barrier(keep=kvkeys + lastkv)
    sb.reset(pmark)
    qk_m = sb.alloc("qk_m", [128, 8, NTOK], BF16)
    cols_tok = sb.alloc("cols_tok", [128, NB, 16], F32)
    gs_rep = sb.alloc("gs_rep", [128, 4, 128], F32)
    emark = sb.mark()

    def g4(name):
        return sb.alloc(name, [4, NB, 128], F32)
    sp_ = g4("sp")
    csp = g4("csp")
    a_ = g4("a")
    cm = g4("cm")
    Mt = g4("M")
    wsq = g4("ws")
    Eq = g4("E")
    gq = g4("g")
    clq = g4("cl")
    ones4 = sb.alloc("ones4", [4, 128], F32)
    chs = sb.alloc("chs", [4, NB, 2], F32)
    chs_all = sb.alloc("chs_all", [4, NB, 8, 2], F32)
    amax_g = sb.alloc("amax_g", [4, 128], F32)
    blast_g = sb.alloc("blast_g", [4, 128], F32)
    mshift = sb.alloc("mshift", [4, 136], F32)
    m127g = sb.alloc("m127g", [4, 128], F32)
    gsg = sb.alloc("gsg", [4, 128], F32)
    mprev = sb.alloc("mprev", [4, NB], F32)
    m127 = sb.alloc("m127", [4, NB], F32)
    nm127 = sb.alloc("nm127", [4, NB], F32)
    biasg = sb.alloc("biasg", [4, NB], F32)
    GIv = GI[:, :].rearrange("p (b k) -> p b k", k=128)
    gk = [(k, t) for t in range(NT) for k in ("gi", "gf")]
    fw.op("dve", lambda e: e.memset(ones4[:], 1.0), w=["ones4"])
    fw.op("dve", lambda e: e.tensor_scalar(out=GI[:, :], in0=GI[:, :], scalar1=15.0, scalar2=None, op0=ALU.mult),
          r=gk, w=["GI"])
    fw.op("act", lambda e: e.activation(out=sp_[:].rearrange("p b k -> p (b k)"), in_=GF[:, :], func=AF.Exp,
                                        scale=-15.0), r=gk, w=["sp"])
    fw.op("act", lambda e: e.activation(out=sp_[:].rearrange("p b k -> p (b k)"),
                                        in_=sp_[:].rearrange("p b k -> p (b k)"), func=AF.Ln,
                                        bias=cst[0:4, CL["one"]:CL["one"] + 1], scale=1.0), r=["sp", "cst"], w=["sp"])
    for j in range(NB):
        fw.op("dve", lambda e, j=j: e.tensor_tensor_scan(out=csp[:, j, :], data0=ones4[:], data1=sp_[:, j, :],
                                                         initial=0.0, op0=ALU.mult, op1=ALU.add),
              r=["sp", "ones4"], w=[("csp", j)])
    cspk = [("csp", j) for j in range(NB)]
    fw.op("dve", lambda e: e.tensor_tensor(out=a_[:], in0=GIv, in1=csp[:], op=ALU.add), r=["GI"] + cspk, w=["a"])
    for j in range(NB):
        fw.op("dve", lambda e, j=j: e.tensor_tensor_scan(out=cm[:, j, :], data0=a_[:, j, :], data1=a_[:, j, :],
                                                         initial=-1.0e30, op0=ALU.max, op1=ALU.max),
              r=["a"], w=[("cm", j)])
    cmk = [("cm", j) for j in range(NB)]
    fw.op("dve", lambda e: e.tensor_scalar(out=chs[:, :, 0:1], in0=csp[:, :, 127:128], scalar1=-1.0, scalar2=None,
                                           op0=ALU.mult), r=cspk, w=["chs0"])
    fw.op("dve", lambda e: e.tensor_copy(out=chs[:, :, 1:2], in_=cm[:, :, 127:128]), r=cmk, w=["chs1"])
    fw.dma("sp", CHS_loc, chs[:].rearrange("p a b -> p (a b)"), r=["chs0", "chs1"], w=["CHS_loc"], stream="io")
    fw.allgather(CHS_loc, CHS_all, r=["CHS_loc"], w=["CHS_all"])
    kv_ag()
    tl_all = sb.alloc("tl_all", [128, 8, 8, NB, 3], F32)
    halo = sb.alloc("halo", [128, 8, NB, 3], F32)
    fw.dma("sp", tl_all[:].rearrange("p r a b c -> p r (a b c)"), TL_all.rearrange("(r p) x -> p r x", p=128),
           r=["TL_all"], w=["tl_all"], stream="io")
    so = CL["sel"]
    fw.op("dve", lambda e: e.tensor_scalar(out=halo[:], in0=tl_all[:, 0], scalar1=cst[:, so:so + 1], scalar2=None,
                                           op0=ALU.mult), r=["tl_all", "cst"], w=["halo"])
    for r_ in range(1, 7):
        fw.op("dve", lambda e, r_=r_: e.scalar_tensor_tensor(out=halo[:], in0=tl_all[:, r_],
                                                             scalar=cst[:, so + r_:so + r_ + 1], in1=halo[:],
                                                             op0=ALU.mult, op1=ALU.add),
              r=["tl_all", "halo", "cst"], w=["halo"])
    if NB > 1:
        fw.op("dve", lambda e: e.scalar_tensor_tensor(out=halo[:, :, 1:NB, :], in0=tl_all[:, 7, :, 0:NB - 1, :],
                                                      scalar=cst[:, so + 7:so + 8], in1=halo[:, :, 1:NB, :],
                                                      op0=ALU.mult, op1=ALU.add),
              r=["tl_all", "halo", "cst"], w=["halo"])
    xp = [sb.alloc(f"xp{i}", [128, NB, 131], F32) for i in range(2)]
    yc = sb.alloc("yc", [128, NB, 128], F32)
    cw = CL["conv_w"]
    cb = CL["conv_b"]
    for ch in range(8):
        s = ch % 2
        fw.dma("sp", xp[s][:, :, 3:131], MQK[ch * 128:(ch + 1) * 128, :].rearrange("p (b k) -> p b k", k=128),
               r=[("mqk", t, ch) for t in range(NT)], w=[("xp", s)], stream="io")
        fw.op("dve", lambda e, s=s, ch=ch: e.tensor_copy(out=xp[s][:, :, 0:3], in_=halo[:, ch]),
              r=["halo", ("xp", s)], w=[("xp", s)])
        fw.op("dve", lambda e, s=s, ch=ch: e.tensor_scalar(out=yc[:], in0=xp[s][:, :, 0:128],
                                                           scalar1=cst[:, cw + ch * 4:cw + ch * 4 + 1],
                                                           scalar2=cst[:, cb + ch:cb + ch + 1], op0=ALU.mult,
                                                           op1=ALU.add), r=[("xp", s), "cst"], w=["yc"])
        for i in range(1, 4):
            fw.op("dve", lambda e, s=s, ch=ch, i=i: e.scalar_tensor_tensor(
                out=yc[:], in0=xp[s][:, :, i:i + 128], scalar=cst[:, cw + ch * 4 + i:cw + ch * 4 + i + 1], in1=yc[:],
                op0=ALU.mult, op1=ALU.add), r=[("xp", s), "yc", "cst"], w=["yc"])
        fw.op("act", lambda e, ch=ch: e.activation(out=qk_m[:, ch, :], in_=yc[:].rearrange("p b k -> p (b k)"),
                                                   func=AF.Silu), r=["yc"], w=[("qk_m", ch)])

    for r_ in range(8):
        fw.dma("sp", chs_all[:, :, r_, :], CHS_all[r_ * 4:(r_ + 1) * 4, :].rearrange("h (j x) -> h j x", x=2),
               r=["CHS_all"], w=[("chs_all", r_)], stream="io")
    cak = [("chs_all", r_) for r_ in range(8)]
    fw.op("dve", lambda e: e.tensor_copy(out=blast_g[:, 0:8 * NB].rearrange("p (j r) -> p j r", r=8), in_=chs_all[:, :, :, 0]),
          r=cak, w=["blast_g"])
    fw.op("dve", lambda e: e.tensor_copy(out=amax_g[:, 0:8 * NB].rearrange("p (j r) -> p j r", r=8), in_=chs_all[:, :, :, 1]),
          r=cak, w=["amax_g"])
    NCH = 8 * NB
    fw.op("dve", lambda e: e.memset(mshift[:], 0.0), w=["mshift"])
    fw.op("dve", lambda e: e.tensor_tensor_scan(out=mshift[:, 1:1 + NCH], data0=amax_g[:, 0:NCH],
                                                data1=blast_g[:, 0:NCH], initial=0.0, op0=ALU.max, op1=ALU.add),
          r=["amax_g", "blast_g", "mshift"], w=["mshift"])
    fw.op("dve", lambda e: e.tensor_tensor(out=m127g[:, 0:NCH], in0=mshift[:, 0:NCH], in1=amax_g[:, 0:NCH],
                                           op=ALU.max), r=["mshift", "amax_g"], w=["m127g"])
    fw.op("dve", lambda e: e.tensor_tensor(out=gsg[:, 0:NCH], in0=mshift[:, 0:NCH], in1=m127g[:, 0:NCH],
                                           op=ALU.subtract), r=["mshift", "m127g"], w=["gsg"])
    fw.op("act", lambda e: e.activation(out=gsg[:, 0:NCH], in_=gsg[:, 0:NCH], func=AF.Exp), r=["gsg"], w=["gsg"])
    for h in range(4):
        fw.op("pe", lambda e, h=h: e.matmul(banks[7][:, 0:NCH], cmat[0:4, 384 + h * 128:384 + (h + 1) * 128],
                                            gsg[:, 0:NCH], start=True, stop=True), r=["gsg", "cmat"], w=[pk(7)])
        fw.op("dve", lambda e, h=h: e.tensor_copy(out=gs_rep[:, h, 0:NCH], in_=banks[7][:, 0:NCH]), r=[pk(7)],
              w=[("gs_rep", h)])
    sc = CL["selc"]
    msv = mshift[:, 0:NCH].rearrange("p (j r) -> p j r", r=8)
    fw.op("dve", lambda e: e.tensor_scalar(out=mprev[:, :], in0=msv[:, :, 0], scalar1=cst[0:4, sc:sc + 1],
                                           scalar2=None, op0=ALU.mult), r=["mshift", "cst"], w=["mprev"])
    for r_ in range(1, 8):
        fw.op("dve", lambda e, r_=r_: e.scalar_tensor_tensor(out=mprev[:, :], in0=msv[:, :, r_],
                                                             scalar=cst[0:4, sc + r_:sc + r_ + 1], in1=mprev[:, :],
                                                             op0=ALU.mult, op1=ALU.add),
              r=["mshift", "mprev", "cst"], w=["mprev"])
    for j in range(NB):
        fw.op("dve", lambda e, j=j: e.tensor_scalar(out=Mt[:, j, :], in0=cm[:, j, :], scalar1=mprev[:, j:j + 1],
                                                    scalar2=None, op0=ALU.max), r=[("cm", j), "mprev"], w=[("M", j)])
    Mk = [("M", j) for j in range(NB)]
    fw.op("dve", lambda e: e.tensor_copy(out=m127[:, :], in_=Mt[:, :, 127]), r=Mk, w=["m127"])
    fw.op("dve", lambda e: e.tensor_scalar(out=nm127[:, :], in0=m127[:, :], scalar1=-1.0, scalar2=None, op0=ALU.mult),
          r=["m127"], w=["nm127"])
    fw.op("dve", lambda e: e.tensor_scalar(out=biasg[:, :], in0=mprev[:, :], scalar1=math.log(QSCALE), scalar2=None,
                                           op0=ALU.add), r=["mprev"], w=["biasg"])
    for j in range(NB):
        fw.op("act", lambda e, j=j: e.activation(out=wsq[:, j, :], in_=a_[:, j, :], func=AF.Exp,
                                                 bias=nm127[:, j:j + 1], scale=1.0), r=["a", "nm127"], w=[("q0", j)])
        fw.op("act", lambda e, j=j: e.activation(out=Eq[:, j, :], in_=Mt[:, j, :], func=AF.Exp,
                                                 bias=m127[:, j:j + 1], scale=-1.0), r=[("M", j), "m127"],
              w=[("q1", j)])
        fw.op("act", lambda e, j=j: e.activation(out=gq[:, j, :], in_=Mt[:, j, :], func=AF.Exp,
                                                 bias=biasg[:, j:j + 1], scale=-1.0), r=[("M", j), "biasg"],
              w=[("q2", j)])
    fw.op("dve", lambda e: e.tensor_tensor(out=clq[:], in0=csp[:], in1=Mt[:], op=ALU.subtract), r=cspk + Mk,
          w=["cl"])
    fw.op("act", lambda e: e.activation(out=clq[:].rearrange("p b k -> p (b k)"),
                                        in_=clq[:].rearrange("p b k -> p (b k)"), func=AF.Exp), r=["cl"], w=["cl"])
    for j in range(NB):
        for q, (T_, key) in enumerate(((wsq, ("q0", j)), (Eq, ("q1", j)), (gq, ("q2", j)), (clq, "cl"))):
            fw.op("pe", lambda e, j=j, q=q, T_=T_: e.transpose(banks[6][:, q * 4:(q + 1) * 4], T_[:, j, :],
                                                               ident_f[0:4, 0:4]),
                  r=[key, "cmat"], w=[("ps6", q)])
        fw.op("dve", lambda e, j=j: e.tensor_copy(out=cols_tok[:, j, :], in_=banks[6][:, 0:16]),
              r=[("ps6", q) for q in range(4)], w=[("cols", j)])

    fw.barrier(keep=kvkeys)
    sb.reset(emark)
    Kb = [sb.alloc("Kb0", [128, 8, NTOK], BF16), None]
    Vb = sb.alloc("Vb", [128, 8, NB, 256], BF16)
    dmark = sb.mark()
    units = [(h, c) for h in range(4) for c in range(2)]

    def load_K(u):
        h, c = units[u]
        s = u % 2
        for r_ in range(8):
            row = r_ * 1024 + (h * 2 + c) * 128
            for t in range(NT):
                fw.dma("sp", Kb[s][:, r_, t * TT:(t + 1) * TT], KT_all[t][row:row + 128, :], r=[("KT_all", t)],
                       w=[("Kb", s, r_, t)], stream="kv")

    def load_V(h):
        for r_ in range(8):
            for t in range(NT):
                fw.dma("sp", Vb[:, r_, t * 4:(t + 1) * 4, :],
                       V_all[t][r_ * TT:(r_ + 1) * TT, h * 256:(h + 1) * 256].rearrange("(j p) e -> p j e", p=128),
                       r=[("V_all", t)], w=[("Vb", r_, t)], stream="kv")
    v_aug = sb.alloc("v_aug", [128, NB, 4, 257], BF16)
    kw = [sb.alloc(f"kw{i}", [128, 128], BF16) for i in range(2)]
    ust = [sb.alloc(f"ust{i}", [128, 4, 257], F32) for i in range(2)]
    psKT = banks[5].bitcast(BF16)

    def load_vaug():
        fw.op("dve", lambda e: e.memset(v_aug[:, :, :, 256:257], 1.0), w=["v_ones"])
        for j in range(NB):
            fw.dma("sp", v_aug[:, j, :, 0:256], MV[j * 128:(j + 1) * 128, :].rearrange("p (h e) -> p h e", e=256),
                   r=[("mv", j // 4, i, j % 4) for i in range(2)], w=[("v_aug", j)], stream="io")
    load_vaug()
    n = 0
    for j in range(NB):
        for h in range(4):
            s = n % 2
            n += 1
            fw.op("pe", lambda e, j=j, h=h: e.transpose(psKT[:, 0:128], qk_m[:, 4 + h, j * 128:(j + 1) * 128],
                                                        ident_b), r=[("qk_m", 4 + h), "cmatb"], w=[pk(5)])
            fw.op("dve", lambda e, j=j, h=h, s=s: e.tensor_scalar(out=kw[s][:, :], in0=psKT[:, 0:128],
                                                                  scalar1=cols_tok[:, j, h:h + 1], scalar2=None,
                                                                  op0=ALU.mult), r=[pk(5), ("cols", j)],
                  w=[("kw", s)])
            fw.op("pe", lambda e, j=j, h=h, s=s: e.matmul(banks[4][:, 0:257], kw[s][:, :], v_aug[:, j, h, :],
                                                          start=True, stop=True),
                  r=[("kw", s), ("v_aug", j), "v_ones"], w=[pk(4)])
            fw.op("act", lambda e, j=j, h=h: e.activation(out=ust[j % 2][:, h, :], in_=banks[4][:, 0:257],
                                                          func=AF.Copy), r=[pk(4)], w=[("ust", j % 2, h)])
        fw.dma("sp", U_loc[j * 512:(j + 1) * 512, :].rearrange("(h p) c -> p h c", p=128), ust[j % 2][:],
               r=[("ust", j % 2, h) for h in range(4)], w=[("U_loc", j)], stream="io")

    fw.allgather(U_loc, U_all, r=[("U_loc", j) for j in range(NB)], w=["U_all"])

    fw.barrier(keep=kvkeys + ["U_all"])
    sb.reset(dmark)
    Kb[1] = sb.alloc("Kb1", [128, 8, NTOK], BF16)
    o1n = sb.alloc("o1n", [128, NG, 2, 512], F32)
    accD = sb.alloc("accD", [128, 512], F32)
    accG = sb.alloc("accG", [128, 512], F32)
    top_ = sb.mark()
    if hole_hi - hole_lo >= 17 * 1024 + 512:
        sb.reset(hole_lo)
    qt = [sb.alloc(f"qt{i}", [128, 512], BF16) for i in range(2)]
    PT = [sb.alloc(f"PT{i}", [128, 512], BF16) for i in range(3)]
    Rl = sb.alloc("Rl", [128, 512], F32)
    dif = sb.alloc("dif", [128, 2, 512], F32)
    sqd = sb.alloc("sqd", [128, 2, 512], BF16)
    rsd = sb.alloc("rsd", [128, 512], F32)
    yst = [sb.alloc(f"yst{i}", [128, 512], BF16) for i in range(2)]
    gsc = sb.alloc("gsc", [128, 8], F32)
    if hole_hi - hole_lo >= 17 * 1024 + 512:
        assert sb.mark() <= hole_hi, (sb.mark(), hole_hi)
        sb.reset(top_)
    amask = cmatb[:, 256:256 + 1024].rearrange("p (r k) -> p r k", k=128)
    an = CL["ahn"]
    fw.op("dve", lambda e: e.tensor_scalar(out=gsc[:, :], in0=cst[:, an:an + 8], scalar1=1.0 - LAM_INIT, scalar2=None,
                                           op0=ALU.mult), r=["cst"], w=["gsc"])
    SBK = [0, 1, 5]
    OB = [[2, 3], [6, 7]]
    ngrp = 0
    u_ag_done = False
    nq = 0
    npt = 0
    nys = 0
    for u, (h, c) in enumerate(units):
        if u == 0:
            load_K(0)
        if c == 0:
            load_V(h)
        if u + 1 < len(units):
            load_K(u + 1)
        if False and not u_ag_done:
            u_ag_done = True
            fw.allgather(U_loc, U_all,
                         r=[("Kb", s_, r_, t) for s_ in range(2) for r_ in range(8) for t in range(NT)] +
                           [("Vb", r_, t) for r_ in range(8) for t in range(NT)], w=["U_all"])
        ks = u % 2
        for g in range(NG):
            qs = nq % 2
            nq += 1
            if nq == 1:
                fw.dma("sp", qt[qs][:, :], QT[(h * 2 + c) * 128:(h * 2 + c + 1) * 128, g * 512:(g + 1) * 512],
                       w=[("qt", qs)], stream="io")
            nxt = (u, g + 1) if g + 1 < NG else ((u + 1, 0) if u + 1 < len(units) else None)
            if nxt is not None:
                (h2, c2), g2 = units[nxt[0]], nxt[1]
                fw.dma("sp", qt[nq % 2][:, :],
                       QT[(h2 * 2 + c2) * 128:(h2 * 2 + c2 + 1) * 128, g2 * 512:(g2 + 1) * 512],
                       w=[("qt", nq % 2)], stream="io")
            nkv = 32 * g + 32
            ob = OB[ngrp % 2]
            ngrp += 1

            def emitS(i, ks=ks, qs=qs, g=g):
                jlo = max(4 * g, i // 8)
                cs = (jlo - 4 * g) * 128
                r_, jp = i % 8, i // 8
                fw.op("pe", lambda e: e.matmul(banks[SBK[i % 3]][:, cs:512], Kb[ks][:, r_, jp * 128:(jp + 1) * 128],
                                               qt[qs][:, cs:512], start=True, stop=True),
                      r=[("Kb", ks, r_, jp // 4), ("qt", qs)], w=[pk(SBK[i % 3])])
            emitS(0)
            emitS(1)
            for i in range(nkv):
                if i + 2 < nkv:
                    emitS(i + 2)
                jlo = max(4 * g, i // 8)
                cs = (jlo - 4 * g) * 128
                r_, jp = i % 8, i // 8
                ps_ = npt % 3
                npt += 1
                fw.op("act", lambda e, i=i, cs=cs, ps_=ps_: e.activation(out=PT[ps_][:, cs:512],
                                                                         in_=banks[SBK[i % 3]][:, cs:512],
                                                                         func=AF.Exp, scale=QSCALE),
                      r=[pk(SBK[i % 3])], w=[("PT", ps_)])
                if i >= 32 * g:
                    fw.op("dve", lambda e, cs=cs, ps_=ps_, r_=r_: e.tensor_tensor(out=PT[ps_][:, cs:cs + 128],
                                                                                  in0=PT[ps_][:, cs:cs + 128],
                                                                                  in1=amask[:, r_, :], op=ALU.mult),
                          r=[("PT", ps_), "cmatb"], w=[("PT", ps_)])

                def av(e, i=i, cs=cs, ps_=ps_, r_=r_, jp=jp, nkv=nkv, ob=ob):
                    for ec in range(2):
                        ins = e.matmul(banks[ob[ec]][:, cs:512], Vb[:, r_, jp, ec * 128:(ec + 1) * 128],
                                       PT[ps_][:, cs:512], start=(i == 0), stop=(i == nkv - 1))
                    return ins
                fw.op("pe", av, r=[("PT", ps_), ("Vb", r_, jp // 4)], w=[pk(ob[0]), pk(ob[1])])
                aeng, acc, akey = (("dve", accD, "accD") if i % 2 == 0 else ("pool", accG, "accG"))
                if i < 2:
                    fw.op(aeng, lambda e, acc=acc, ps_=ps_: e.tensor_copy(out=acc[:, :], in_=PT[ps_][:, :]),
                          r=[("PT", ps_)], w=[akey])
                else:
                    fw.op(aeng, lambda e, acc=acc, ps_=ps_, cs=cs: e.tensor_tensor(out=acc[:, cs:512],
                                                                                  in0=acc[:, cs:512],
                                                                                  in1=PT[ps_][:, cs:512], op=ALU.add),
                          r=[("PT", ps_), akey], w=[akey])

            def lm(e):
                e.matmul(banks[4][:, :], ones_f, accD[:, :], start=True, stop=False)
                return e.matmul(banks[4][:, :], ones_f, accG[:, :], start=False, stop=True)
            fw.op("pe", lm, r=["accD", "accG", "cmat"], w=[pk(4)])
            fw.op("dve", lambda e: e.reciprocal(out=Rl[:, :], in_=banks[4][:, :]), r=[pk(4)], w=["Rl"])
            if c == 0:
                for ec in range(2):
                    fw.op("dve", lambda e, ec=ec, g=g: e.tensor_tensor(out=o1n[:, g, ec, :], in0=banks[ob[ec]][:, :],
                                                                       in1=Rl[:, :], op=ALU.mult),
                          r=[pk(ob[ec]), "Rl"], w=[("o1n", g, ec)])
            else:
                for ec in range(2):
                    fw.op("dve", lambda e, ec=ec: e.tensor_tensor(out=dif[:, ec, :], in0=banks[ob[ec]][:, :],
                                                                  in1=Rl[:, :], op=ALU.mult),
                          r=[pk(ob[ec]), "Rl"], w=[("dif", ec)])
                    fw.op("dve", lambda e, ec=ec, g=g: e.scalar_tensor_tensor(out=dif[:, ec, :], in0=dif[:, ec, :],
                                                                              scalar=neglam, in1=o1n[:, g, ec, :],
                                                                              op0=ALU.mult, op1=ALU.add),
                          r=[("dif", ec), ("o1n", g, ec), "lamt"], w=[("dif", ec)])
                rmsnorm_stats(dif, 512, rsd, sqd, 256, 4, [("dif", 0), ("dif", 1)], "a")
                for ec in range(2):
                    ys = nys % 2
                    nys += 1
                    fw.op("dve", lambda e, ec=ec, ys=ys, h=h: e.scalar_tensor_tensor(
                        out=yst[ys][:, :], in0=dif[:, ec, :], scalar=gsc[:, h * 2 + ec:h * 2 + ec + 1], in1=rsd[:, :],
                        op0=ALU.mult, op1=ALU.mult), r=[("dif", ec), "arstd", "gsc"], w=[("yst", ys)])
                    row = h * 256 + ec * 128
                    fw.dma("sp", YA[row:row + 128, g * 512:(g + 1) * 512], yst[ys][:, :], r=[("yst", ys)],
                           w=[("YA", g, h * 2 + ec)], stream="io")

    fw.barrier(keep=["U_all"])
    sb.reset(emark)
    v_aug = sb.alloc("v_aug2", [128, NB, 4, 257], BF16)
    csel = sb.alloc("csel", [128, NB, 4, 257], BF16)
    mo_sb = sb.alloc("mo_sb", [128, 8, NTOK], BF16)
    ur = [sb.alloc(f"ur{i}", [128, 4, 257], F32) for i in range(4)]
    crun = [sb.alloc(f"crun{i}", [128, 4 * 257], F32) for i in range(2)]
    selI = sb.alloc("selI", [128, 8, 128], F32)
    load_vaug()
    for ch in range(8):
        fw.dma("sp", mo_sb[:, ch, :], MO[ch * 128:(ch + 1) * 128, :], r=[("mo", t, ch) for t in range(NT)],
               w=[("mo_sb", ch)], stream="io")
    Uv = U_all.rearrange("(r j h p) c -> r j p h c", r=8, j=NB, h=4, p=128)
    P2 = [sb.alloc(f"P2{i}", [128, 128], BF16) for i in range(2)]
    tmpi = [sb.alloc(f"tmpi{i}", [128, 257], F32) for i in range(2)]
    tot = [sb.alloc(f"tot{i}", [128, 257], F32) for i in range(2)]
    sm = [sb.alloc(f"sm{i}", [128, 8], F32) for i in range(2)]
    junk = [sb.alloc(f"junk{i}", [128, 256], F32) for i in range(2)]
    hn = [sb.alloc(f"hn{i}", [128, 256], F32) for i in range(2)]
    ybs = [sb.alloc(f"ybs{i}", [128, 8, 128], BF16) for i in range(2)]
    mn = CL["mhn"]

    def e2_unit(j, h):
        p = (j * 4 + h) % 2
        bS, bI, bC, bT = 0, 1, 2, 4 + p
        jc = slice(j * 128, (j + 1) * 128)
        tt, ti, s_, jk, hh = tot[p], tmpi[p], sm[p], junk[p], hn[p]
        fw.op("pe", lambda e: e.matmul(banks[bS][:, 0:128], qk_m[:, 4 + h, jc], qk_m[:, h, jc], start=True,
                                       stop=True), r=[("qk_m", h), ("qk_m", 4 + h)], w=[pk(bS)])
        yield
        fw.op("dve", lambda e: e.scalar_tensor_tensor(out=P2[p][:, :], in0=banks[bS][:, 0:128],
                                                      scalar=cols_tok[:, j, h:h + 1], in1=trimask, op0=ALU.mult,
                                                      op1=ALU.mult), r=[pk(bS), ("cols", j), "cmat"], w=[("P2", p)])
        yield
        fw.op("pe", lambda e: e.matmul(banks[bI][:, 0:257], P2[p][:, :], v_aug[:, j, h, :], start=True, stop=True),
              r=[("P2", p), ("v_aug", j), "v_ones"], w=[pk(bI)])
        fw.op("pe", lambda e: e.matmul(banks[bC][:, 0:257], qk_m[:, h, jc], csel[:, j, h, :], start=True, stop=True),
              r=[("qk_m", h), ("csel", j, 3), ("csel", j, 7), ("csel", j, 6)], w=[pk(bC)])
        yield
        fw.op("dve", lambda e: e.tensor_scalar(out=ti[:, :], in0=banks[bI][:, 0:257],
                                               scalar1=cols_tok[:, j, 4 + h:5 + h], scalar2=None, op0=ALU.mult),
              r=[pk(bI), ("cols", j)], w=[("tmpi", p)])
        yield
        fw.op("dve", lambda e: e.scalar_tensor_tensor(out=tt[:, :], in0=banks[bC][:, 0:257],
                                                      scalar=cols_tok[:, j, 8 + h:9 + h], in1=ti[:, :], op0=ALU.mult,
                                                      op1=ALU.add), r=[pk(bC), ("cols", j), ("tmpi", p)],
              w=[("tot", p)])
        yield
        fw.op("act", lambda e: e.activation(out=s_[:, 6:7], in_=tt[:, 256:257], func=AF.Abs), r=[("tot", p)],
              w=[("sm6", p)])
        yield
        fw.op("dve", lambda e: e.tensor_tensor(out=s_[:, 0:1], in0=s_[:, 6:7], in1=cols_tok[:, j, 12 + h:13 + h],
                                               op=ALU.max), r=[("sm6", p), ("cols", j)], w=[("sm0", p)])
        yield
        fw.op("dve", lambda e: e.reciprocal(out=s_[:, 1:2], in_=s_[:, 0:1]), r=[("sm0", p)], w=[("sm1", p)])
        yield
        fw.op("act", lambda e: e.activation(out=jk[:, :], in_=tt[:, 0:256], func=AF.Square, scale=s_[:, 1:2]),
              r=[("tot", p), ("sm1", p)], w=[("junk", p)])
        yield
        fw.op("dve", lambda e: e.tensor_reduce(out=s_[:, 2:3], in_=jk[:, :], axis=mybir.AxisListType.X, op=ALU.add),
              r=[("junk", p)], w=[("sm2", p)])
        yield
        fw.op("act", lambda e: e.activation(out=s_[:, 3:4], in_=s_[:, 2:3], func=AF.Sqrt, bias=ccol("eps"),
                                            scale=1.0 / 256.0), r=[("sm2", p), "cst"], w=[("sm3", p)])
        yield
        fw.op("dve", lambda e: e.reciprocal(out=s_[:, 4:5], in_=s_[:, 3:4]), r=[("sm3", p)], w=[("sm4", p)])
        yield
        fw.op("dve", lambda e: e.tensor_tensor(out=s_[:, 5:6], in0=s_[:, 4:5], in1=s_[:, 1:2], op=ALU.mult),
              r=[("sm4", p), ("sm1", p)], w=[("sm5", p)])
        yield
        fw.op("dve", lambda e: e.tensor_scalar(out=hh[:, :], in0=tt[:, 0:256], scalar1=s_[:, 5:6], scalar2=None,
                                               op0=ALU.mult), r=[("tot", p), ("sm5", p)], w=[("hn", p)])
        yield

        def tr(e):
            e.transpose(banks[bT][:, 0:128], hh[:, 0:128], ident_f)
            return e.transpose(banks[bT][:, 128:256], hh[:, 128:256], ident_f)
        fw.op("pe", tr, r=[("hn", p), "cmat"], w=[pk(bT)])
        yield
        for ec in range(2):
            q = h * 2 + ec
            fw.op("dve", lambda e, ec=ec, q=q: e.scalar_tensor_tensor(
                out=ybs[j % 2][:, q, :], in0=banks[bT][:, ec * 128:(ec + 1) * 128], scalar=cst[:, mn + q:mn + q + 1],
                in1=mo_sb[:, q, jc], op0=ALU.mult, op1=ALU.mult), r=[pk(bT), ("mo_sb", q), "cst"],
                w=[("ybs", j % 2, q)])
            yield

    def e2_row(j):
        for h in range(4):
            yield from e2_unit(j, h)
        fw.dma("sp", YB[:, j * 128:(j + 1) * 128].rearrange("(c p) k -> p c k", p=128), ybs[j % 2][:],
               r=[("ybs", j % 2, q) for q in range(8)], w=[("YB", j)], stream="io")
        yield

    pend = []

    def pump(n):
        for _ in range(n):
            while pend:
                try:
                    next(pend[0])
                    break
                except StopIteration:
                    pend.pop(0)

    for r_ in range(8):
        fw.op("dve", lambda e, r_=r_: e.tensor_scalar(out=selI[:, r_, :], in0=ident_f, scalar1=cst[:, sc + r_:sc + r_ + 1],
                                                      scalar2=None, op0=ALU.mult), r=["cmat", "cst"], w=[("selI", r_)])
    fw.op("dve", lambda e: e.memset(crun[0][:], 0.0), w=[("crun", 0, h) for h in range(4)])
    CB = [(0, 512, 3), (512, 1024, 7), (1024, 1028, 6)]
    for t_ in range(NCH):
        j, r_ = t_ // 8, t_ % 8
        us = t_ % 4
        ca, cb_ = t_ % 2, (t_ + 1) % 2
        fw.dma("sp", ur[us][:], Uv[r_, j], r=["U_all"], w=[("ur", us)], stream="u")

        def selmm(e, r_=r_, ca=ca):
            for (c0, c1, bk) in CB:
                ins = e.matmul(banks[bk][:, 0:c1 - c0], selI[:, r_, :], crun[ca][:, c0:c1], start=(r_ == 0),
                               stop=(r_ == 7))
            return ins
        fw.op("pe", selmm, r=[("crun", ca, h) for h in range(4)] + [("selI", r_)], w=[pk(3), pk(7), pk(6)])
        pump(1)
        if r_ == 7:
            cj = csel[:, j].rearrange("p h c -> p (h c)")
            for (c0, c1, bk) in CB:
                fw.op("act", lambda e, c0=c0, c1=c1, bk=bk, cj=cj: e.activation(out=cj[:, c0:c1],
                                                                                in_=banks[bk][:, 0:c1 - c0],
                                                                                func=AF.Copy),
                      r=[pk(bk)], w=[("csel", j, bk)])
            pend.append(e2_row(j))
        if t_ < NCH - 1:
            for h in range(4):
                fw.op("dve", lambda e, h=h, t_=t_, us=us, ca=ca, cb_=cb_: e.scalar_tensor_tensor(
                    out=crun[cb_][:, h * 257:(h + 1) * 257], in0=crun[ca][:, h * 257:(h + 1) * 257],
                    scalar=gs_rep[:, h, t_:t_ + 1], in1=ur[us][:, h, :], op0=ALU.mult, op1=ALU.add),
                    r=[("crun", ca, h), ("ur", us), ("gs_rep", h)], w=[("crun", cb_, h)])
                pump(2)
    pump(1 << 30)

    fw.barrier()
    sb.reset(pmark)
    b = alloc_ffn()
    yt = sb.alloc("yt", [128, 16, TT], BF16)
    ost = [sb.alloc(f"ost{i}", [128, TT], F32) for i in range(2)]
    plans = []
    for t in range(NT):
        kp = len(b.rg.tasks)
        for o in range(DC):
            def ld(slot, o=o):
                fw.dma("pool", b.wgu[slot][:, 0:16, 0, :], wout[:, o * 128:(o + 1) * 128].rearrange("(c p) n -> p c n", p=128),
                       w=[("wgu", slot, 0)], stream="wq")
            b.rg.add(ld)
        k0, k1 = plan_ffn(b, 1)
        plans.append((kp, k0, k1))
    for t in range(NT):
        kp, k0, k1 = plans[t]
        tc0 = t * TT
        for c in range(DC):
            fw.dma("sp", b.xt[:, c, :], X1[c * 128:(c + 1) * 128, tc0:tc0 + TT], r=[("X1", t, c)], w=[("x", c)],
                   stream="io")
        for q in range(8):
            fw.dma("sp", yt[:, q, :], YA[q * 128:(q + 1) * 128, tc0:tc0 + TT], r=[("YA", t, q)], w=[("yt", q)],
                   stream="io")
            fw.dma("sp", yt[:, 8 + q, :], YB[q * 128:(q + 1) * 128, tc0:tc0 + TT],
                   r=[("YB", j) for j in range(t * 4, t * 4 + 4)], w=[("yt", 8 + q)], stream="io")
        ytk = [("yt", q) for q in range(16)]
        for o in range(DC):
            slot = b.rg.use(kp + o)
            bank = 4 + o % 2

            def mm(e, bank=bank, slot=slot):
                for q in range(16):
                    ins = e.matmul(banks[bank][:, 0:TT], b.wgu[slot][:, q, 0, :], yt[:, q, :], start=(q == 0),
                                   stop=(q == 15))
                return ins
            fw.op("pe", mm, r=ytk + [("wgu", slot, 0)], w=[pk(bank)])
            fw.op("dve", lambda e, o=o, bank=bank: e.tensor_tensor(out=b.xt[:, o, :], in0=banks[bank][:, 0:TT],
                                                                   in1=b.xt[:, o, :], op=ALU.add),
                  r=[pk(bank), ("x", o)], w=[("x", o)])
        norm_to_h(b, "g_ffn2")
        run_ffn(b, k0, k1)
        xk = [("x", c) for c in range(DC)]
        rmsnorm_stats(b.xt, TT, b.rstd, b.sq, D, 6, xk, "n")
        for c in range(DC):
            s = c % 2
            fw.op("dve", lambda e, c=c, s=s: e.scalar_tensor_tensor(out=ost[s][:, :], in0=b.xt[:, c, :],
                                                                    scalar=ccol("g_final", c), in1=b.rstd[:, :],
                                                                    op0=ALU.mult, op1=ALU.mult),
                  r=[("x", c), "nrstd", "cst"], w=[("ost", s)])
            fw.dma("sp", outT[c * 128:(c + 1) * 128, tc0:tc0 + TT], ost[s][:, :], r=[("ost", s)], w=[("out", t, c)],
                   stream="io")
    fw.barrier()
    return nc, dbg


def host_prep(cfg, inp, c):
    S, D, DFF, NTOK, NB, DC = cfg.S, cfg.D, cfg.DFF, cfg.NTOK, cfg.NB, cfg.DC
    CL = cst_layout(DC)
    x = inp["x"][0]
    xc = x.reshape(NB, NCORES, 128, D)[:, c].reshape(NTOK, D)
    m = {"xT": np.ascontiguousarray(xc.T)}
    cst = np.zeros((128, CL["_n"]), np.float32)

    def pc(v):
        return np.asarray(v, np.float32).reshape(-1, 128).T
    cst[:, CL["g_ffn1"]:CL["g_ffn1"] + DC] = pc(inp["ffn1_norm"][0])
    cst[:, CL["g_mix"]:CL["g_mix"] + DC] = pc(inp["mix_norm"][0])
    cst[:, CL["g_ffn2"]:CL["g_ffn2"] + DC] = pc(inp["ffn2_norm"][0])
    cst[:, CL["g_final"]:CL["g_final"] + DC] = pc(inp["final_norm"])
    cw = np.asarray(inp["conv_w"][0][:, 0, :], np.float32)
    cst[:, CL["conv_w"]:CL["conv_w"] + 32] = cw.reshape(4, 8, 128).transpose(2, 1, 0).reshape(128, 32)
    cst[:, CL["conv_b"]:CL["conv_b"] + 8] = pc(inp["conv_b"][0])
    cst[:, CL["ahn"]:CL["ahn"] + 8] = pc(inp["attn_head_norm"][0].reshape(-1))
    cst[:, CL["mhn"]:CL["mhn"] + 8] = pc(inp["mlstm_head_norm"][0].reshape(-1))
    for i, k in enumerate(["lam_q1", "lam_k1", "lam_q2", "lam_k2"]):
        cst[:, CL["lam"] + i] = inp[k][0]
    sel = np.zeros(8, np.float32)
    if c == 0:
        sel[7] = 1.0
    else:
        sel[c - 1] = 1.0
    cst[:, CL["sel"]:CL["sel"] + 8] = sel[None, :]
    selc = np.zeros(8, np.float32)
    selc[c] = 1.0
    cst[:, CL["selc"]:CL["selc"] + 8] = selc[None, :]
    cst[0:4, CL["bgate"]] = inp["b_igate"][0]
    cst[0:4, CL["bgate"] + 1] = inp["b_fgate"][0]
    cst[:, CL["eps"]] = EPS
    cst[:, CL["one"]] = 1.0
    m["cst"] = cst
    cmat = np.zeros((128, 3 * 128 + 512), np.float32)
    cmat[:, 0:128] = np.eye(128, dtype=np.float32)
    cmat[:, 128:256] = np.triu(np.ones((128, 128), np.float32)) * QSCALE
    cmat[:, 256:384] = 1.0
    for h in range(4):
        cmat[h, 384 + h * 128:384 + (h + 1) * 128] = 1.0
    m["cmat"] = cmat
    cmb = np.zeros((128, 10 * 128), np.float32)
    cmb[:, 0:128] = np.eye(128, dtype=np.float32)
    cmb[:, 128:256] = 1.0
    for r in range(8):
        if r < c:
            cmb[:, 256 + r * 128:256 + (r + 1) * 128] = 1.0
        elif r == c:
            cmb[:, 256 + r * 128:256 + (r + 1) * 128] = np.triu(np.ones((128, 128), np.float32))
    m["cmatb"] = cmb
    return m


_CACHE = {}


def run(cfg, inp, debug=False):
    key = (cfg.S, cfg.D, cfg.DFF, debug)
    if key not in _CACHE:
        _CACHE[key] = build(cfg, debug)
    nc, dbg = _CACHE[key]
    shared = {
        "wg1": np.ascontiguousarray(inp["ffn1_w_gate"][0]), "wu1": np.ascontiguousarray(inp["ffn1_w_up"][0]),
        "wd1": np.ascontiguousarray(inp["ffn1_w_down"][0]), "wg2": np.ascontiguousarray(inp["ffn2_w_gate"][0]),
        "wu2": np.ascontiguousarray(inp["ffn2_w_up"][0]), "wd2": np.ascontiguousarray(inp["ffn2_w_down"][0]),
        "win": np.ascontiguousarray(inp["w_in"][0]), "wout": np.ascontiguousarray(inp["w_out"][0]),
    }
    in_maps = []
    for c in range(NCORES):
        m = host_prep(cfg, inp, c)
        m.update(shared)
        in_maps.append(m)
    res = run_bass_kernel_spmd(nc, in_maps, core_ids=list(range(NCORES)))
    NB, NTOK, D = cfg.NB, cfg.NTOK, cfg.D
    out = np.empty((NB, NCORES, 128, D), np.float32)
    for c in range(NCORES):
        out[:, c] = res.results[c]["outT"].T.reshape(NB, 128, D)
    return out.reshape(1, cfg.S, D), res


def kernel(**inputs):
    inp = {k: np.asarray(v) for k, v in inputs.items()}
    cfg = Cfg(inp["x"].shape[1], inp["x"].shape[2], inp["ffn1_w_gate"].shape[2])
    out, _ = run(cfg, inp)
    return out
```
